# Optimizing a Trainium2 kernel written in Bass

```python
import jax, jax.numpy as jnp
from jax import lax
import numpy as np

D_MODEL = 4096
BATCH = 2
SEQ = 4096
DEPTH = 2

CHUNK = 64
Q_BLOCK = 128
HEAD_DIM = 128
N_HEADS = D_MODEL // HEAD_DIM
N_HEADS_A = N_HEADS // 2
N_HEADS_B = N_HEADS - N_HEADS_A
N_HEADS_C = N_HEADS
WIDTH_A = N_HEADS_A * HEAD_DIM
WIDTH_B = N_HEADS_B * HEAD_DIM
WIDTH_C = N_HEADS_C * HEAD_DIM
LEFT_CHUNKS = 8
BAND_CHUNKS = LEFT_CHUNKS + 1
REL_CLIP = 128
N_REL = 2 * REL_CLIP + 1
RMS_EPS = 1e-6
N_EVEN = (DEPTH + 1) // 2
N_ODD = DEPTH // 2
IN_EVEN = 4 * WIDTH_A + 4 * WIDTH_B + N_HEADS_A
IN_ODD = 4 * WIDTH_C

kernel_name = "hybrid_fox_chunkrel_stickbreak_sandwich"


def rms_norm(x, g):
    xf = x.astype(jnp.float32)
    y = xf * lax.rsqrt(jnp.mean(xf * xf, axis=-1, keepdims=True) + RMS_EPS)
    return (y * g.astype(jnp.float32)).astype(x.dtype)


def split_cols(proj, widths):
    outs, off = [], 0
    for w in widths:
        outs.append(proj[..., off:off + w])
        off += w
    return outs


def heads(t, n_heads):
    b, s, _ = t.shape
    return t.reshape(b, s, n_heads, HEAD_DIM)


def forgetting_attention(q, k, v, log_f):
    b, s_len, h, dh = q.shape
    scale = dh ** -0.5
    c = jnp.transpose(jnp.cumsum(log_f, axis=1), (0, 2, 1))
    outs = []
    for i in range(s_len // Q_BLOCK):
        q0, q1 = i * Q_BLOCK, (i + 1) * Q_BLOCK
        logits = jnp.einsum('bqhd,bkhd->bhqk', q[:, q0:q1], k[:, :q1],
                            preferred_element_type=jnp.float32) * scale
        decay = c[:, :, q0:q1, None] - c[:, :, None, :q1]
        tq = jnp.arange(q0, q1)[:, None]
        tk = jnp.arange(q1)[None, :]
        logits = jnp.where(tk <= tq, logits + decay, -jnp.inf)
        p = jax.nn.softmax(logits, axis=-1).astype(v.dtype)
        outs.append(jnp.einsum('bhqk,bkhd->bqhd', p, v[:, :q1]))
    return jnp.concatenate(outs, axis=1)


def chunked_relpos_attention(q, k, v, rel_bias):
    b, s_len, h, dh = q.shape
    scale = dh ** -0.5
    nc = s_len // CHUNK
    band = BAND_CHUNKS * CHUNK
    qc = q.reshape(b, nc, CHUNK, h, dh)
    pad = ((0, 0), (LEFT_CHUNKS, 0), (0, 0), (0, 0), (0, 0))
    kc = jnp.pad(k.reshape(b, nc, CHUNK, h, dh), pad)
    vc = jnp.pad(v.reshape(b, nc, CHUNK, h, dh), pad)
    logits = jnp.concatenate(
        [jnp.einsum('bnqhd,bnkhd->bhnqk', qc, kc[:, j:j + nc],
                    preferred_element_type=jnp.float32) for j in range(BAND_CHUNKS)],
        axis=-1) * scale
    qi = jnp.arange(CHUNK)[:, None]
    km = jnp.arange(band)[None, :]
    rel = jnp.clip(LEFT_CHUNKS * CHUNK + qi - km, -REL_CLIP, REL_CLIP) + REL_CLIP
    bias = rel_bias.astype(jnp.float32)[:, rel]
    src_chunk = jnp.arange(nc)[:, None] - LEFT_CHUNKS + km // CHUNK
    valid = (src_chunk >= 0)[None, None, :, None, :]
    logits = jnp.where(valid, logits + bias[None, :, None], -jnp.inf)
    p = jax.nn.softmax(logits, axis=-1).astype(v.dtype)
    out = jnp.einsum('bhnqk,bnkhd->bnqhd', p[..., :CHUNK], vc[:, 0:nc])
    for j in range(1, BAND_CHUNKS):
        out = out + jnp.einsum('bhnqk,bnkhd->bnqhd',
                               p[..., j * CHUNK:(j + 1) * CHUNK], vc[:, j:j + nc])
    return out.reshape(b, s_len, h, dh)


def stick_breaking_attention(q, k, v):
    b, s_len, h, dh = q.shape
    scale = dh ** -0.5
    outs = []
    for i in range(s_len // Q_BLOCK):
        q0, q1 = i * Q_BLOCK, (i + 1) * Q_BLOCK
        z = jnp.einsum('bqhd,bkhd->bhqk', q[:, q0:q1], k[:, :q1],
                       preferred_element_type=jnp.float32) * scale
        tq = jnp.arange(q0, q1)[:, None]
        tk = jnp.arange(q1)[None, :]
        causal = tk < tq
        log_beta = jax.nn.log_sigmoid(z)
        log_one_minus = jnp.where(causal, jax.nn.log_sigmoid(-z), 0.0)
        tail = lax.cumsum(log_one_minus, axis=3, reverse=True) - log_one_minus
        a = jnp.where(causal, jnp.exp(log_beta + tail), 0.0).astype(v.dtype)
        outs.append(jnp.einsum('bhqk,bkhd->bqhd', a, v[:, :q1]))
    return jnp.concatenate(outs, axis=1)


def even_mixer(h, w_in, b_f, rel_bias, w_out):
    b, s_len, _ = h.shape
    proj = jnp.einsum('bsd,de->bse', h, w_in)
    aq, ak, av, ag, bq, bk, bv, bg, af = split_cols(
        proj, [WIDTH_A] * 4 + [WIDTH_B] * 4 + [N_HEADS_A])
    log_f = jax.nn.log_sigmoid((af + b_f).astype(jnp.float32))
    oa = forgetting_attention(heads(aq, N_HEADS_A), heads(ak, N_HEADS_A),
                              heads(av, N_HEADS_A), log_f).reshape(b, s_len, WIDTH_A)
    ob = chunked_relpos_attention(heads(bq, N_HEADS_B), heads(bk, N_HEADS_B),
                                  heads(bv, N_HEADS_B), rel_bias).reshape(b, s_len, WIDTH_B)
    mixed = jnp.concatenate([oa * jax.nn.silu(ag), ob * jax.nn.silu(bg)], axis=-1)
    return jnp.einsum('bse,ed->bsd', mixed, w_out)


def odd_mixer(h, w_in, w_out):
    b, s_len, _ = h.shape
    proj = jnp.einsum('bsd,de->bse', h, w_in)
    cq, ck, cv, cg = split_cols(proj, [WIDTH_C] * 4)
    oc = stick_breaking_attention(heads(cq, N_HEADS_C), heads(ck, N_HEADS_C),
                                  heads(cv, N_HEADS_C)).reshape(b, s_len, WIDTH_C)
    return jnp.einsum('bse,ed->bsd', oc * jax.nn.silu(cg), w_out)


def setup_inputs(seed: int = 0) -> dict:
    key = jax.random.key(seed)
    ks = jax.random.split(key, 10)
    fan = D_MODEL ** -0.5
    x = jax.random.normal(ks[0], (BATCH, SEQ, D_MODEL), jnp.float32)
    norm_pre = 1.0 + 0.05 * jax.random.normal(ks[1], (DEPTH, D_MODEL), jnp.float32)
    norm_post = 1.0 + 0.05 * jax.random.normal(ks[2], (DEPTH, D_MODEL), jnp.float32)
    w_in_even = jax.random.normal(ks[3], (N_EVEN, D_MODEL, IN_EVEN), jnp.float32) * fan
    b_f_even = jax.random.uniform(ks[4], (N_EVEN, N_HEADS_A), jnp.float32, 1.0, 4.0)
    rel_bias_even = 0.1 * jax.random.normal(ks[5], (N_EVEN, N_HEADS_B, N_REL), jnp.float32)
    w_out_even = jax.random.normal(ks[6], (N_EVEN, WIDTH_A + WIDTH_B, D_MODEL), jnp.float32) * (WIDTH_A + WIDTH_B) ** -0.5
    w_in_odd = jax.random.normal(ks[7], (N_ODD, D_MODEL, IN_ODD), jnp.float32) * fan
    w_out_odd = jax.random.normal(ks[8], (N_ODD, WIDTH_C, D_MODEL), jnp.float32) * WIDTH_C ** -0.5
    return {"x": x, "norm_pre": norm_pre, "norm_post": norm_post,
            "w_in_even": w_in_even, "b_f_even": b_f_even,
            "rel_bias_even": rel_bias_even, "w_out_even": w_out_even,
            "w_in_odd": w_in_odd, "w_out_odd": w_out_odd}


def reference(x, norm_pre, norm_post, w_in_even, b_f_even, rel_bias_even, w_out_even, w_in_odd, w_out_odd):
    for layer in range(DEPTH):
        h = rms_norm(x, norm_pre[layer])
        if layer % 2 == 0:
            e = layer // 2
            y = even_mixer(h, w_in_even[e], b_f_even[e], rel_bias_even[e], w_out_even[e])
        else:
            o = layer // 2
            y = odd_mixer(h, w_in_odd[o], w_out_odd[o])
        x = x + rms_norm(y, norm_post[layer])
    return x
```

```python
import contextlib
import numpy as np
import ml_dtypes
import concourse.bass as bass
import concourse.mybir as mybir
from concourse.bass_utils import run_bass_kernel_spmd

F32 = mybir.dt.float32
BF16 = mybir.dt.bfloat16
I32 = mybir.dt.int32
AF = mybir.ActivationFunctionType
ALU = mybir.AluOpType

D_MODEL = 4096
SEQ = 4096
BATCH = 2
KC = D_MODEL // 128
HD = 128
EPS = 1e-6
NEG = -1.0e30

ENGS = ("pe", "act", "dve", "pool", "sp")


class Op:
    __slots__ = ("eng", "emit", "reads", "writes", "dma_sem", "deps", "signaled", "sig_val", "waits")

    def __init__(self, eng, emit, reads, writes, dma_sem):
        self.eng = eng
        self.emit = emit
        self.reads = reads
        self.writes = writes
        self.dma_sem = dma_sem
        self.deps = []
        self.signaled = False
        self.sig_val = 0
        self.waits = []


def _okey(o):
    return ("dma", o.dma_sem) if o.dma_sem is not None else ("eng", o.eng)


class Prog:
    def __init__(self, nc):
        self.nc = nc
        self.ops = {e: [] for e in ENGS}
        self.all_ops = []
        self.last_writer = {}
        self.readers = {}
        self.dma_sem_names = []
        self.dma_last = {}
        self.stack = contextlib.ExitStack()
        self.regs = {}

    def reg(self, eng, val):
        if val not in self.regs:
            self.regs[val] = eng.to_reg(val)
        return self.regs[val]

    def sb(self, name, shape, dt, stack=None):
        return (stack or self.stack).enter_context(self.nc.sbuf_tensor(name, shape, dt))

    def ps(self, name, shape, dt, stack=None):
        return (stack or self.stack).enter_context(self.nc.psum_tensor(name, shape, dt))

    def op(self, eng, emit, reads=(), writes=(), dma_sem=None):
        o = Op(eng, emit, tuple(reads), tuple(writes), dma_sem)
        deps = {}
        for k in o.reads:
            w = self.last_writer.get(k)
            if w is not None:
                deps[id(w)] = w
        for k in o.writes:
            w = self.last_writer.get(k)
            if w is not None:
                deps[id(w)] = w
            for r in self.readers.get(k, {}).values():
                deps[id(r)] = r
        if dma_sem is not None:
            if dma_sem not in self.dma_last:
                self.dma_sem_names.append(dma_sem)
            prev = self.dma_last.get(dma_sem)
            if prev is not None:
                deps[id(prev)] = prev
            self.dma_last[dma_sem] = o
        o.deps = [d for d in deps.values() if d is not o]
        for k in o.reads:
            self.readers.setdefault(k, {})[_okey(o)] = o
        for k in o.writes:
            self.last_writer[k] = o
            self.readers[k] = {}
        self.all_ops.append(o)
        self.ops[eng].append(o)
        return o

    def pe(self, emit, reads=(), writes=()):
        return self.op("pe", emit, reads, writes)

    def act(self, emit, reads=(), writes=()):
        return self.op("act", emit, reads, writes)

    def dve(self, emit, reads=(), writes=()):
        return self.op("dve", emit, reads, writes)

    def pool(self, emit, reads=(), writes=()):
        return self.op("pool", emit, reads, writes)

    def dma(self, eng, sem, emit, reads=(), writes=()):
        return self.op(eng, emit, reads, writes, dma_sem=sem)

    def barrier(self):
        lasts = []
        for e in ENGS:
            for o in reversed(self.ops[e]):
                if o.dma_sem is None:
                    lasts.append(o)
                    break
        lasts += list(self.dma_last.values())
        for e in ENGS:
            o = Op(e, lambda eng: eng.nop(), (), (), None)
            o.deps = [d for d in lasts]
            self.all_ops.append(o)
            self.ops[e].append(o)
        self.last_writer = {}
        self.readers = {}

    def finalize(self, final_wait_sems=()):
        nc = self.nc
        for o in self.all_ops:
            need = []
            for d in o.deps:
                if d.dma_sem is None and o.dma_sem is None and d.eng == o.eng:
                    if o.eng == "pe":
                        continue
                    if not any(k in d.writes for k in o.reads):
                        continue
                need.append(d)
            o.deps = need
            for d in need:
                if d.dma_sem is None:
                    d.signaled = True
        cnt = {e: 0 for e in ENGS}
        dcnt = {}
        for o in self.all_ops:
            if o.dma_sem is not None:
                dcnt[o.dma_sem] = dcnt.get(o.dma_sem, 0) + 16
                o.sig_val = dcnt[o.dma_sem]
            elif o.signaled:
                cnt[o.eng] += 1
                o.sig_val = cnt[o.eng]
        seen = {e: {} for e in ENGS}
        for o in self.all_ops:
            req = {}
            for d in o.deps:
                key = _okey(d)
                if d.sig_val > req.get(key, 0):
                    req[key] = d.sig_val
            s = seen[o.eng]
            for key, v in req.items():
                if s.get(key, 0) >= v:
                    continue
                s[key] = v
                o.waits.append((key, v))
        sems = {}
        for e in ENGS:
            sems[("eng", e)] = self.stack.enter_context(nc.semaphore("s_" + e))
        for n in self.dma_sem_names:
            sems[("dma", n)] = self.stack.enter_context(nc.semaphore("d_" + n))
        finals = [(("dma", n), dcnt[n]) for n in final_wait_sems]
        engmap = {"pe": "tensor", "act": "scalar", "dve": "vector", "pool": "gpsimd", "sp": "sync"}
        with nc.Block() as block:
            for e in ENGS:
                ops = self.ops[e]
                fin = finals if e == "sp" else []
                if not ops and not fin:
                    continue

                def body(eng, ops=ops, e=e, fin=fin):
                    for o in ops:
                        for key, v in o.waits:
                            eng.wait_ge(sems[key], v)
                        inst = o.emit(eng)
                        if o.dma_sem is not None:
                            inst.then_inc(sems[("dma", o.dma_sem)], 16)
                        elif o.signaled:
                            inst.then_inc(sems[("eng", e)], 1)
                    for key, v in fin:
                        eng.wait_ge(sems[key], v)

                getattr(block, engmap[e])(body)

    def close(self):
        self.stack.close()


def make_consts(P):
    C = {}
    idi = P.sb("c_idi", [128, 128], I32)
    C["idf"] = P.sb("c_idf", [128, 128], F32)
    C["idb"] = P.sb("c_idb", [128, 128], BF16)
    C["ones"] = P.sb("c_ones", [128, 128], BF16)
    C["tri"] = P.sb("c_tri", [128, 128], BF16)
    P.pool(lambda e: e.iota(idi[:], pattern=[[-1, 128]], base=0, channel_multiplier=1), writes=["c_idi"])
    P.dve(lambda e: e.tensor_scalar(out=C["idf"][:], in0=idi[:], scalar1=0, scalar2=None, op0=ALU.is_equal),
          reads=["c_idi"], writes=["c_idf"])
    P.dve(lambda e: e.tensor_copy(out=C["idb"][:], in_=C["idf"][:]), reads=["c_idf"], writes=["c_idb"])
    P.dve(lambda e: e.memset(C["ones"][:], 1.0), writes=["c_ones"])
    P.dve(lambda e: e.tensor_scalar(out=C["tri"][:], in0=idi[:], scalar1=0, scalar2=None, op0=ALU.is_ge),
          reads=["c_idi"], writes=["c_tri"])
    return C


def emit_T_block(P, C, hs, hs_keys, gcol, gkey, dst, dst_keys, pT, pT_keys):
    for kc in range(KC):
        pb = kc % len(pT)
        for j in range(4):
            P.pe(lambda e, kc=kc, j=j, pb=pb: e.transpose(out=pT[pb][:, j * 128:(j + 1) * 128],
                                                         in_=hs[j][:, kc * 128:(kc + 1) * 128], identity=C["idf"][:]),
                 reads=list(hs_keys[j]) + ["c_idf"], writes=[pT_keys[pb]])
        if kc % 2 == 0:
            P.act(lambda e, kc=kc, pb=pb: e.activation(out=dst[:, kc, :], in_=pT[pb][:], func=AF.Copy,
                                                       scale=gcol[:, kc:kc + 1]),
                  reads=[pT_keys[pb], gkey], writes=[dst_keys[kc]])
        else:
            P.dve(lambda e, kc=kc, pb=pb: e.tensor_scalar(out=dst[:, kc, :], in0=pT[pb][:], scalar1=gcol[:, kc:kc + 1],
                                                          scalar2=None, op0=ALU.mult),
                  reads=[pT_keys[pb], gkey], writes=[dst_keys[kc]])


def emit_rstd(P, ptag, ss, rs):
    P.dve(lambda e: e.tensor_scalar(out=ss[:, 1:2], in0=ss[:, 0:1], scalar1=1.0 / D_MODEL, scalar2=EPS,
                                    op0=ALU.mult, op1=ALU.add), reads=[f"{ptag}ss"], writes=[f"{ptag}ss2"])
    P.act(lambda e: e.activation(out=rs[:, 0:1], in_=ss[:, 1:2], func=AF.Sqrt),
          reads=[f"{ptag}ss2"], writes=[f"{ptag}rs0"])
    P.dve(lambda e: e.reciprocal(out=rs[:, 1:2], in_=rs[:, 0:1]), reads=[f"{ptag}rs0"], writes=[f"{ptag}rs"])


def emit_rmsnorm_T(P, C, x_d, gcol_d, n_tt, hTb, sink, stk, ptag):
    xt = [P.sb(f"{ptag}xt{i}", [128, D_MODEL], F32, stk) for i in range(2)]
    hs = [P.sb(f"{ptag}hs{i}", [128, D_MODEL], F32, stk) for i in range(4)]
    junk = P.sb(f"{ptag}junk", [128, D_MODEL], BF16, stk)
    gcol = P.sb(f"{ptag}gcol", [128, KC], F32, stk)
    ss = P.sb(f"{ptag}ss", [128, 2], F32, stk)
    rs = P.sb(f"{ptag}rs", [128, 2], F32, stk)
    pT = [P.ps(f"{ptag}pT{i}", [128, 512], F32, stk) for i in range(4)]
    P.dma("sp", f"{ptag}gcol", lambda e: e.dma_start(out=gcol[:], in_=gcol_d), writes=[f"{ptag}gcol"])
    for tt in range(n_tt):
        tb, j = divmod(tt, 4)
        slot = tb % 2
        xs = tt % 2
        P.dma("sp", f"{ptag}xt{xs}",
              lambda e, tt=tt, xs=xs: e.dma_start(out=xt[xs][:], in_=x_d[tt * 128:(tt + 1) * 128, :]),
              writes=[f"{ptag}xt{xs}"])
        P.act(lambda e, xs=xs: e.activation(out=junk[:], in_=xt[xs][:], func=AF.Square, accum_out=ss[:, 0:1]),
              reads=[f"{ptag}xt{xs}"], writes=[f"{ptag}junk", f"{ptag}ss"])
        emit_rstd(P, ptag, ss, rs)
        P.dve(lambda e, xs=xs, j=j: e.tensor_scalar(out=hs[j][:], in0=xt[xs][:], scalar1=rs[:, 1:2], scalar2=None,
                                                    op0=ALU.mult),
              reads=[f"{ptag}xt{xs}", f"{ptag}rs"], writes=[(f"{ptag}hs", j)])
        if j == 3:
            keys = [("hTbk", slot, kc) for kc in range(KC)]
            emit_T_block(P, C, hs, [[(f"{ptag}hs", jj)] for jj in range(4)], gcol, f"{ptag}gcol",
                         hTb[slot], keys, pT, [f"{ptag}pT{i}" for i in range(4)])
            sink(tb, slot, keys)


def emit_mixer(P, C, T, head_types, x_d, g_d, w_d, mix_d, hT_d, waf_d=None, bf_d=None, bias_d=None, tag="m",
                phase1=True):
    NH = len(head_types)
    NTB = T // 512
    NKT = T // 128
    NA = sum(1 for h in head_types if h == "A")
    has_A = NA > 0
    has_B = "B" in head_types
    has_C = "C" in head_types
    scale = HD ** -0.5

    hTb = [P.sb(f"{tag}hTb{i}", [128, KC, 512], BF16) for i in range(2)]

    if phase1:
        stk1 = contextlib.ExitStack()

        def sink(tb, slot, keys):
            P.dma("pool", f"{tag}hTs{slot}", lambda e: e.dma_start(out=hT_d[tb], in_=hTb[slot][:]),
                  reads=keys, writes=[("hTd", tb)])

        emit_rmsnorm_T(P, C, x_d, g_d, NKT, hTb, sink, stk1, tag + "1")
        P.barrier()
        stk1.close()

    wb = [P.sb(f"{tag}wb{c}", [128, KC, 128], BF16) for c in range(4)]
    qT = P.sb(f"{tag}qT", [128, T], BF16)
    kT = P.sb(f"{tag}kT", [128, T], BF16)
    sgT = P.sb(f"{tag}sgT", [128, T], BF16)
    vtok = P.sb(f"{tag}vtok", [128, NKT, 128], BF16)
    vTb = P.sb(f"{tag}vTb", [128, 512], BF16)
    mixs = [P.sb(f"{tag}mixs{i}", [128, 512], BF16) for i in range(2)]
    psP = [P.ps(f"{tag}psP{i}", [128, 512], F32) for i in range(2)]
    psT = P.ps(f"{tag}psT", [128, 4, 128], BF16)
    psZ = [P.ps(f"{tag}psZ{i}", [128, 512], F32) for i in range(2)]
    psX = [P.ps(f"{tag}psX{i}", [128, 512], F32) for i in range(2)]
    psO = P.ps(f"{tag}psO", [128, 512], F32)
    if has_A or has_B:
        lg = [P.sb(f"{tag}lg{i}", [128, 512], F32) for i in range(2)]
        Pt = [P.sb(f"{tag}Pt{i}", [128, 512], BF16) for i in range(3)]
        rden = P.sb(f"{tag}rden", [128, 512], F32)
        onrm = P.sb(f"{tag}onrm", [128, 512], F32)
    if has_A:
        waf = P.sb(f"{tag}waf", [128, KC, NA], BF16)
        bfc = P.sb(f"{tag}bfc", [NA, 2], F32)
        e1 = P.sb(f"{tag}e1", [NA, 512], F32)
        spb = P.sb(f"{tag}spb", [NA, 512], F32)
        csp = P.sb(f"{tag}csp", [NA, T], F32)
        seli = P.sb(f"{tag}seli", [NA, 128], I32)
        nsel = [P.sb(f"{tag}nsel{a}", [NA, 128], F32) for a in range(NA)]
        cpos = P.sb(f"{tag}cpos", [128, NKT, NA], F32)
        Cb = [P.sb(f"{tag}Cb{i}", [128, 512], F32) for i in range(2)]
        P.dma("pool", f"{tag}waf", lambda e: e.dma_start(out=waf[:], in_=waf_d), writes=["waf"])
        P.dma("sp", f"{tag}bfc", lambda e: e.dma_start(out=bfc[:, 0:1], in_=bf_d), writes=["bfc0"])
        P.dve(lambda e: e.tensor_scalar(out=bfc[:, 1:2], in0=bfc[:, 0:1], scalar1=-1.0, scalar2=None, op0=ALU.mult),
              reads=["bfc0"], writes=["bfc"])
        for a in range(NA):
            P.pool(lambda e, a=a: e.iota(seli[:], pattern=[[0, 128]], base=-a, channel_multiplier=1),
                   writes=["seli"])
            P.dve(lambda e, a=a: e.tensor_scalar(out=nsel[a][:], in0=seli[:], scalar1=0, scalar2=-1.0,
                                                 op0=ALU.is_equal, op1=ALU.mult),
                  reads=["seli"], writes=[("nsel", a)])
    if has_B:
        biasT = [P.sb(f"{tag}biasT{i}", [128, 5 * 128], F32) for i in range(2)]
    if has_C:
        Et = [P.sb(f"{tag}E{i}", [128, 512], F32) for i in range(4)]
        Lt = [P.sb(f"{tag}L{i}", [128, 512], BF16) for i in range(2)]
        Dm = [P.sb(f"{tag}Dm{i}", [128, 512], F32) for i in range(2)]
        At = [P.sb(f"{tag}At{i}", [128, 512], BF16) for i in range(2)]
        Acc = P.sb(f"{tag}Acc", [128, 512], BF16)

    cnt = {"blk": 0, "z": 0, "s": 0, "mix": 0, "r": 0}

    def load_w(hs):
        for c in range(4):
            P.dma("pool", f"{tag}wb{c}", lambda e, c=c: e.dma_start(out=wb[c][:], in_=w_d[hs, c]),
                  writes=[("wb", c)])

    def load_hT(tb):
        slot = cnt["blk"] % 2
        cnt["blk"] += 1
        P.dma("sp", f"{tag}hTl{slot}", lambda e: e.dma_start(out=hTb[slot][:], in_=hT_d[tb]),
              reads=[("hTd", tb)], writes=[("hTb", slot)])
        return slot

    def mix_out(hs, q0, ncols, emit_mix, reads):
        ms = cnt["mix"] % 2
        cnt["mix"] += 1
        emit_mix(mixs[ms], ms, reads)
        P.dma("sp", f"{tag}mixo{ms}", lambda e: e.dma_start(out=mix_d[hs, :, q0:q0 + ncols], in_=mixs[ms][:, 0:ncols]),
              reads=[("mixs", ms)], writes=[("mixd", hs, q0)])


    obanks = [(psO, "psO"), (psP[0], ("psP", 0))]
    dbanks = [(psX[0], ("psX", 0)), (psP[1], ("psP", 1))]
    rbanks = [(psX[0], ("psX", 0)), (psX[1], ("psX", 1)), (psP[1], ("psP", 1))]
    st = {"n": 0, "q": 0}

    def finish_q(hs, t, ob, db, normalize):
        q0, qi = t["q0"], t["qi"]
        ms = cnt["mix"] % 2
        cnt["mix"] += 1
        if normalize:
            P.dve(lambda e: e.reciprocal(out=rden[:], in_=db[0][:]), reads=[db[1]], writes=["rden"])
            P.dve(lambda e: e.tensor_tensor(out=onrm[:], in0=ob[0][:], in1=rden[:], op=ALU.mult),
                  reads=[ob[1], "rden"], writes=["onrm"])
            P.pool(lambda e: e.tensor_tensor(out=mixs[ms][:], in0=onrm[:], in1=sgT[:, q0:q0 + 512], op=ALU.mult),
                   reads=["onrm", ("sg", qi)], writes=[("mixs", ms)])
        else:
            P.dve(lambda e: e.tensor_tensor(out=mixs[ms][:], in0=ob[0][:], in1=sgT[:, q0:q0 + 512], op=ALU.mult),
                  reads=[ob[1], ("sg", qi)], writes=[("mixs", ms)])
        P.dma("sp", f"{tag}mixo{ms}", lambda e: e.dma_start(out=mix_d[hs, :, q0:q0 + 512], in_=mixs[ms][:]),
              reads=[("mixs", ms)], writes=[("mixd", hs, q0)])

    def stream_AB(hs, tiles):
        N = len(tiles)
        base = st["n"]
        st["n"] += N
        for t in tiles:
            if t["first"]:
                st["q"] += 1
            t["ob"] = obanks[st["q"] % 2]
            t["db"] = dbanks[st["q"] % 2]
            t["cbs"] = st["q"] % 2

        def s_z(i):
            t = tiles[i]
            n = base + i
            zb = n % 2
            kt, c0, c1, q0 = t["kt"], t["c0"], t["c1"], t["q0"]
            if t["kind"] == "A" and t["first"]:
                a, cbs, qi = t["a"], t["cbs"], t["qi"]
                P.pe(lambda e: e.matmul(psX[1][:], lhsT=nsel[a][:], rhs=csp[:, q0:q0 + 512], start=True, stop=True),
                     reads=[("nsel", a), ("csp", qi)], writes=[("psX", 1)])
                P.act(lambda e: e.activation(out=Cb[cbs][:], in_=psX[1][:], func=AF.Copy),
                      reads=[("psX", 1)], writes=[("Cb", cbs)])
            P.pe(lambda e: e.matmul(psZ[zb][:, c0:c1], lhsT=kT[:, kt * 128:(kt + 1) * 128], rhs=qT[:, q0 + c0:q0 + c1],
                                    start=True, stop=True),
                 reads=[("kT", kt // 4), ("qT", t["qi"])], writes=[("psZ", zb)])

        def s_lg(i):
            t = tiles[i]
            n = base + i
            zb = n % 2
            sb_ = n % 2
            kt, c0, c1 = t["kt"], t["c0"], t["c1"]
            if t["kind"] == "A":
                a, cbs = t["a"], t["cbs"]
                P.dve(lambda e: e.scalar_tensor_tensor(out=lg[sb_][:, c0:c1], in0=psZ[zb][:, c0:c1],
                                                       scalar=cpos[:, kt, a:a + 1], in1=Cb[cbs][:, c0:c1],
                                                       op0=ALU.add, op1=ALU.add),
                      reads=[("psZ", zb), "cpos", ("Cb", cbs)], writes=[("lg", sb_)])
            else:
                bs, dlo = t["bs"], t["dlo"]
                P.dve(lambda e: e.tensor_tensor(out=lg[sb_][:, c0:c1], in0=psZ[zb][:, c0:c1],
                                                in1=biasT[bs][:, dlo * 128:dlo * 128 + (c1 - c0)], op=ALU.add),
                      reads=[("psZ", zb), ("biasT", bs)], writes=[("lg", sb_)])
            for mk in t["masks"]:
                if mk[0] == "tri":
                    cc = mk[1]
                    P.pool(lambda e, cc=cc: e.affine_select(
                        out=lg[sb_][:, cc:cc + 128], in_=lg[sb_][:, cc:cc + 128], pattern=[[1, 128]],
                        compare_op=ALU.is_ge, fill=P.reg(e, NEG), base=0, channel_multiplier=-1),
                        reads=[("lg", sb_)], writes=[("lg", sb_)])
                else:
                    _, p0, p1, x0, x1 = mk
                    P.pool(lambda e, p0=p0, p1=p1, x0=x0, x1=x1: e.memset(lg[sb_][p0:p1, x0:x1], NEG),
                           reads=[("lg", sb_)], writes=[("lg", sb_)])

        def s_p(i):
            t = tiles[i]
            n = base + i
            sb_ = n % 2
            pb = n % 3
            c0, c1 = t["c0"], t["c1"]
            P.act(lambda e: e.activation(out=Pt[pb][:, c0:c1], in_=lg[sb_][:, c0:c1], func=AF.Exp),
                  reads=[("lg", sb_)], writes=[("Pt", pb)])

        def s_pv(i):
            t = tiles[i]
            n = base + i
            pb = n % 3
            kt, c0, c1 = t["kt"], t["c0"], t["c1"]
            ob, db = t["ob"], t["db"]
            P.pe(lambda e: e.matmul(ob[0][:, c0:c1], lhsT=vtok[:, kt, :], rhs=Pt[pb][:, c0:c1],
                                    start=t["first"], stop=t["last"]),
                 reads=[("v", kt // 4), ("Pt", pb)], writes=[ob[1]])
            P.pe(lambda e: e.matmul(db[0][:, c0:c1], lhsT=C["ones"][:], rhs=Pt[pb][:, c0:c1],
                                    start=t["first"], stop=t["last"]),
                 reads=["c_ones", ("Pt", pb)], writes=[db[1]])
            if t["last"]:
                finish_q(hs, t, ob, db, True)

        for i in range(-3, N):
            if 0 <= i + 3 < N:
                s_z(i + 3)
            if 0 <= i + 2 < N:
                s_lg(i + 2)
            if 0 <= i + 1 < N:
                s_p(i + 1)
            if 0 <= i:
                s_pv(i)

    def stream_C(hs, tiles):
        N = len(tiles)
        base = st["n"]
        st["n"] += N
        for t in tiles:
            if t["first"]:
                st["q"] += 1
            t["ob"] = obanks[st["q"] % 2]

        def s_z(i):
            t = tiles[i]
            zb = (base + i) % 2
            kt, c0, q0 = t["kt"], t["c0"], t["q0"]
            P.pe(lambda e: e.matmul(psZ[zb][:, c0:512], lhsT=kT[:, kt * 128:(kt + 1) * 128], rhs=qT[:, q0 + c0:q0 + 512],
                                    start=True, stop=True),
                 reads=[("kT", kt // 4), ("qT", t["qi"])], writes=[("psZ", zb)])

        def s_E(i):
            t = tiles[i]
            n = base + i
            c0 = t["c0"]
            P.act(lambda e: e.activation(out=Et[n % 4][:, c0:512], in_=psZ[n % 2][:, c0:512], func=AF.Exp),
                  reads=[("psZ", n % 2)], writes=[("E", n % 4)])

        def s_L(i):
            t = tiles[i]
            n = base + i
            c0 = t["c0"]
            ls = n % 2
            rbk, rkey = rbanks[n % 3]
            P.act(lambda e: e.activation(out=Lt[ls][:, c0:512], in_=Et[n % 4][:, c0:512], func=AF.Ln, bias=1.0),
                  reads=[("E", n % 4)], writes=[("L", ls)])
            if t["diag"]:
                P.pool(lambda e: e.affine_select(
                    out=Lt[ls][:, c0:c0 + 128], in_=Lt[ls][:, c0:c0 + 128], pattern=[[1, 128]],
                    compare_op=ALU.is_gt, fill=P.reg(e, 0.0), base=0, channel_multiplier=-1),
                    reads=[("L", ls)], writes=[("L", ls)])
            if t["first"]:
                P.pool(lambda e: e.memset(Acc[:], 0.0), writes=["Acc"])
            P.pe(lambda e: e.matmul(rbk[:, c0:512], lhsT=C["tri"][:], rhs=Lt[ls][:, c0:512], start=True, stop=t["first"]),
                 reads=["c_tri", ("L", ls)], writes=[rkey])
            if not t["first"]:
                P.pe(lambda e: e.matmul(rbk[:, c0:512], lhsT=C["ones"][:], rhs=Acc[:, c0:512], start=False, stop=True),
                     reads=["c_ones", "Acc"], writes=[rkey])
            if not t["last"]:
                P.dve(lambda e: e.tensor_tensor(out=Acc[:, c0:512], in0=Acc[:, c0:512], in1=Lt[ls][:, c0:512], op=ALU.add),
                      reads=["Acc", ("L", ls)], writes=["Acc"])

        def s_D(i):
            t = tiles[i]
            n = base + i
            c0 = t["c0"]
            ds_ = n % 2
            rbk, rkey = rbanks[n % 3]
            P.act(lambda e: e.activation(out=Dm[ds_][:, c0:512], in_=rbk[:, c0:512], func=AF.Exp, scale=-1.0),
                  reads=[rkey], writes=[("Dm", ds_)])
            P.dve(lambda e: e.tensor_tensor(out=At[ds_][:, c0:512], in0=Et[n % 4][:, c0:512], in1=Dm[ds_][:, c0:512], op=ALU.mult),
                  reads=[("E", n % 4), ("Dm", ds_)], writes=[("At", ds_)])
            if t["diag"]:
                P.pool(lambda e: e.affine_select(
                    out=At[ds_][:, c0:c0 + 128], in_=At[ds_][:, c0:c0 + 128], pattern=[[1, 128]],
                    compare_op=ALU.is_gt, fill=P.reg(e, 0.0), base=0, channel_multiplier=-1),
                    reads=[("At", ds_)], writes=[("At", ds_)])

        def s_PV(i):
            t = tiles[i]
            n = base + i
            c0, kt = t["c0"], t["kt"]
            ds_ = n % 2
            ob = t["ob"]
            P.pe(lambda e: e.matmul(ob[0][:, c0:512], lhsT=vtok[:, kt, :], rhs=At[ds_][:, c0:512], start=t["first"], stop=t["last"]),
                 reads=[("v", kt // 4), ("At", ds_)], writes=[ob[1]])
            if t["last"]:
                finish_q(hs, t, ob, None, False)

        for i in range(-3, N):
            if 0 <= i + 3 < N:
                s_z(i + 3)
            if 0 <= i + 2 < N:
                s_E(i + 2)
            if 0 <= i:
                s_D(i)
            if 0 <= i + 2 < N:
                s_L(i + 2)
            if 0 <= i:
                s_PV(i)

    a_idx = 0
    b_idx = 0
    load_w(0)
    pending = [load_hT(0)]
    for hs, ht in enumerate(head_types):
        with_af = has_A and hs == 0
        for tb in range(NTB):
            slot = pending.pop(0)
            if tb + 1 < NTB:
                pending.append(load_hT(tb + 1))
            elif hs + 1 < NH:
                pending.append(load_hT(0))
            cs = slice(tb * 512, (tb + 1) * 512)
            for c in range(4):
                pb = c % 2
                for kc in range(KC):
                    P.pe(lambda e, c=c, kc=kc, pb=pb, slot=slot: e.matmul(
                        psP[pb][:], lhsT=wb[c][:, kc, :], rhs=hTb[slot][:, kc, :], start=(kc == 0), stop=(kc == KC - 1)),
                        reads=[("wb", c), ("hTb", slot)], writes=[("psP", pb)])
                if c == 0:
                    P.act(lambda e, pb=pb, cs=cs: e.activation(out=qT[:, cs], in_=psP[pb][:], func=AF.Copy, scale=scale),
                          reads=[("psP", pb)], writes=[("qT", tb)])
                elif c == 1:
                    P.dve(lambda e, pb=pb, cs=cs: e.tensor_copy(out=kT[:, cs], in_=psP[pb][:]),
                          reads=[("psP", pb)], writes=[("kT", tb)])
                elif c == 2:
                    P.dve(lambda e, pb=pb: e.tensor_copy(out=vTb[:], in_=psP[pb][:]),
                          reads=[("psP", pb)], writes=["vTb"])
                    for i in range(4):
                        P.pe(lambda e, i=i: e.transpose(out=psT[:, i, :], in_=vTb[:, i * 128:(i + 1) * 128],
                                                        identity=C["idb"][:]),
                             reads=["vTb", "c_idb"], writes=["psT"])
                    P.act(lambda e, tb=tb: e.activation(out=vtok[:, tb * 4:(tb + 1) * 4, :], in_=psT[:], func=AF.Copy),
                          reads=["psT"], writes=[("v", tb)])
                else:
                    P.act(lambda e, pb=pb, cs=cs: e.activation(out=sgT[:, cs], in_=psP[pb][:], func=AF.Silu),
                          reads=[("psP", pb)], writes=[("sg", tb)])
            if with_af:
                for kc in range(KC):
                    P.pe(lambda e, kc=kc, slot=slot: e.matmul(
                        psO[0:NA, :], lhsT=waf[:, kc, :], rhs=hTb[slot][:, kc, :], start=(kc == 0), stop=(kc == KC - 1)),
                        reads=["waf", ("hTb", slot)], writes=["psO"])
                P.act(lambda e: e.activation(out=e1[:], in_=psO[0:NA, :], func=AF.Exp, scale=-1.0, bias=bfc[:, 1:2]),
                      reads=["psO", "bfc"], writes=["e1"])
                P.act(lambda e: e.activation(out=spb[:], in_=e1[:], func=AF.Ln, bias=1.0),
                      reads=["e1"], writes=["spb"])
                if tb == 0:
                    P.dve(lambda e, cs=cs: e.tensor_tensor_scan(out=csp[:, cs], data0=spb[:], data1=spb[:], initial=0.0,
                                                                op0=ALU.add, op1=ALU.bypass),
                          reads=["spb"], writes=[("csp", tb)])
                else:
                    P.dve(lambda e, cs=cs, tb=tb: e.tensor_tensor_scan(out=csp[:, cs], data0=spb[:], data1=spb[:],
                                                                       initial=csp[:, tb * 512 - 1:tb * 512],
                                                                       op0=ALU.add, op1=ALU.bypass),
                          reads=["spb", ("csp", tb - 1)], writes=[("csp", tb)])
        if with_af:
            for kt in range(NKT):
                P.pe(lambda e, kt=kt: e.matmul(psX[0][:, kt * NA:(kt + 1) * NA], lhsT=csp[:, kt * 128:(kt + 1) * 128],
                                               rhs=C["idf"][0:NA, 0:NA], start=True, stop=True),
                     reads=[("csp", kt // 4), "c_idf"], writes=[("psX", 0)])
            P.dve(lambda e: e.tensor_copy(out=cpos[:].rearrange("p k a -> p (k a)"), in_=psX[0][:, 0:NKT * NA]),
                  reads=[("psX", 0)], writes=["cpos"])
        if hs + 1 < NH:
            load_w(hs + 1)

        if ht == "A":
            a = a_idx
            a_idx += 1
            tiles = []
            for qi in range(NTB):
                nk = 4 * qi + 4
                for kt in range(nk):
                    m = kt - 4 * qi
                    c0 = 128 * max(m, 0)
                    tiles.append(dict(kind="A", a=a, qi=qi, kt=kt, c0=c0, c1=512, q0=qi * 512,
                                      masks=([("tri", c0)] if m >= 0 else []),
                                      first=(kt == 0), last=(kt == nk - 1)))
            stream_AB(hs, tiles)
        elif ht == "B":
            b = b_idx
            b_idx += 1
            bs = b % 2
            P.dma("sp", f"{tag}biasT{bs}", lambda e, b=b, bs=bs: e.dma_start(out=biasT[bs][:], in_=bias_d[b]),
                  writes=[("biasT", bs)])
            tiles = []
            for qg in range(NTB):
                jf = max(0, 4 * qg - 4)
                for j in range(jf, 4 * qg + 4):
                    m_lo = max(j, 4 * qg)
                    m_hi = min(j + 4, 4 * qg + 3)
                    c0 = (m_lo - 4 * qg) * 128
                    c1 = (m_hi - 4 * qg + 1) * 128
                    masks = []
                    for m in range(m_lo, m_hi + 1):
                        col = (m - 4 * qg) * 128
                        if m - j == 0:
                            masks.append(("blk", 64, 128, col, col + 64))
                        if m - j == 4:
                            masks.append(("blk", 0, 64, col + 64, col + 128))
                    tiles.append(dict(kind="B", bs=bs, dlo=m_lo - j, qi=qg, kt=j, c0=c0, c1=c1, q0=qg * 512,
                                      masks=masks, first=(j == jf), last=(j == 4 * qg + 3)))
            stream_AB(hs, tiles)
        else:
            tiles = []
            for qi in range(NTB):
                nk = 4 * qi + 4
                for kt in range(nk - 1, -1, -1):
                    m = kt - 4 * qi
                    c0 = 128 * max(m, 0)
                    tiles.append(dict(qi=qi, kt=kt, c0=c0, c1=512, q0=qi * 512, diag=(m >= 0),
                                      first=(kt == nk - 1), last=(kt == 0)))
            stream_C(hs, tiles)

    return [f"{tag}mixo0", f"{tag}mixo1"]


def build_mixer(T, head_types, ext_hT=False):
    nc = bass.Bass("TRN2", target_bir_lowering=False)
    NH = len(head_types)
    NA = sum(1 for h in head_types if h == "A")
    NB = sum(1 for h in head_types if h == "B")
    x_d = g_d = None
    if ext_hT:
        hT_d = nc.dram_tensor("hT", [T // 512, 128, KC, 512], BF16, kind="ExternalInput").ap()
    else:
        x_d = nc.dram_tensor("x", [T, D_MODEL], F32, kind="ExternalInput").ap()
        g_d = nc.dram_tensor("gcol", [128, KC], F32, kind="ExternalInput").ap()
        hT_d = nc.dram_tensor("hT_scr", [T // 512, 128, KC, 512], BF16, kind="Internal").ap()
    w_d = nc.dram_tensor("w", [NH, 4, 128, KC, 128], F32, kind="ExternalInput").ap()
    mix_d = nc.dram_tensor("mixT", [NH, 128, T], BF16, kind="ExternalOutput").ap()
    waf_d = bf_d = bias_d = None
    if NA:
        waf_d = nc.dram_tensor("waf", [128, KC, NA], F32, kind="ExternalInput").ap()
        bf_d = nc.dram_tensor("bf", [NA, 1], F32, kind="ExternalInput").ap()
    if NB:
        bias_d = nc.dram_tensor("biasT", [NB, 128, 5 * 128], F32, kind="ExternalInput").ap()
    P = Prog(nc)
    C = make_consts(P)
    fin = emit_mixer(P, C, T, head_types, x_d, g_d, w_d, mix_d, hT_d, waf_d, bf_d, bias_d, phase1=not ext_hT)
    P.finalize(final_wait_sems=fin)
    P.close()
    return nc


def emit_outproj(P, C, TO, mix_d, wo_d, x_d, g_d, out_d, tag="o", gnext_d=None, hT_out=None):
    NPASS = TO // 512
    mx = P.sb(f"{tag}mx", [128, KC, 512], BF16)
    wsl = [P.sb(f"{tag}wsl{i}", [128, 16, 512], BF16) for i in range(2)]
    y = [P.sb(f"{tag}y{i}", [128, D_MODEL], F32) for i in range(4)]
    gb = P.sb(f"{tag}gb", [128, D_MODEL], F32)
    xt = P.sb(f"{tag}xt", [128, D_MODEL], F32)
    junk = P.sb(f"{tag}junk", [128, D_MODEL], BF16)
    ss = P.sb(f"{tag}ss", [128, 2], F32)
    rs = P.sb(f"{tag}rs", [128, 2], F32)
    psY = [P.ps(f"{tag}psY{i}", [128, 512], F32) for i in range(8)]
    P.dma("sp", f"{tag}gb", lambda e: e.dma_start(out=gb[:], in_=g_d[:, :]), writes=[f"{tag}gb"])
    mxk = [(f"{tag}mx", kc) for kc in range(KC)]
    if hT_out is not None:
        gcol = P.sb(f"{tag}gcol", [128, KC], F32)
        ss2 = P.sb(f"{tag}n_ss", [128, 2], F32)
        rs2 = P.sb(f"{tag}n_rs", [128, 2], F32)
        P.dma("sp", f"{tag}gcol", lambda e: e.dma_start(out=gcol[:], in_=gnext_d), writes=[f"{tag}gcol"])
    wcnt = 0
    ev = 0
    for ps_ in range(NPASS):
        t0 = ps_ * 512
        P.dma("sp", f"{tag}mx", lambda e, t0=t0: e.dma_start(out=mx[:], in_=mix_d[:, :, t0:t0 + 512]),
              writes=mxk)
        for nb in range(8):
            for kh in range(2):
                ws = wcnt % 2
                wcnt += 1
                P.dma("pool", f"{tag}wsl{ws}", lambda e, nb=nb, kh=kh, ws=ws: e.dma_start(out=wsl[ws][:], in_=wo_d[nb, kh]),
                      writes=[(f"{tag}wsl", ws)])
                for tt in range(4):
                    pb = (nb % 2) * 4 + tt
                    for kc in range(16):
                        P.pe(lambda e, tt=tt, kc=kc, kh=kh, ws=ws, pb=pb: e.matmul(
                            psY[pb][:], lhsT=mx[:, kh * 16 + kc, tt * 128:(tt + 1) * 128], rhs=wsl[ws][:, kc, :],
                            start=(kh == 0 and kc == 0), stop=(kh == 1 and kc == 15)),
                            reads=[mxk[kh * 16 + kc], (f"{tag}wsl", ws)], writes=[(f"{tag}psY", pb)])
            for tt in range(4):
                pb = (nb % 2) * 4 + tt
                if ev % 2 == 0:
                    P.act(lambda e, tt=tt, nb=nb, pb=pb: e.activation(out=y[tt][:, nb * 512:(nb + 1) * 512], in_=psY[pb][:], func=AF.Copy),
                          reads=[(f"{tag}psY", pb)], writes=[(f"{tag}y", tt, nb)])
                else:
                    P.dve(lambda e, tt=tt, nb=nb, pb=pb: e.tensor_copy(out=y[tt][:, nb * 512:(nb + 1) * 512], in_=psY[pb][:]),
                          reads=[(f"{tag}psY", pb)], writes=[(f"{tag}y", tt, nb)])
                ev += 1
        for tt in range(4):
            r0 = t0 + tt * 128
            yk = [(f"{tag}y", tt, nb) for nb in range(8)]
            P.dma("sp", f"{tag}xt", lambda e, r0=r0: e.dma_start(out=xt[:], in_=x_d[r0:r0 + 128, :]), writes=[f"{tag}xt"])
            P.act(lambda e, tt=tt: e.activation(out=junk[:], in_=y[tt][:], func=AF.Square, accum_out=ss[:, 0:1]),
                  reads=yk, writes=[f"{tag}junk", f"{tag}ss"])
            emit_rstd(P, tag, ss, rs)
            P.dve(lambda e, tt=tt: e.scalar_tensor_tensor(out=y[tt][:], in0=y[tt][:], scalar=rs[:, 1:2], in1=gb[:],
                                                          op0=ALU.mult, op1=ALU.mult),
                  reads=yk + [f"{tag}rs", f"{tag}gb"], writes=yk)
            P.pool(lambda e, tt=tt: e.tensor_tensor(out=xt[:], in0=xt[:], in1=y[tt][:], op=ALU.add),
                   reads=yk + [f"{tag}xt"], writes=[f"{tag}xt"])
            P.dma("sp", f"{tag}out", lambda e, r0=r0: e.dma_start(out=out_d[r0:r0 + 128, :], in_=xt[:]),
                  reads=[f"{tag}xt"], writes=[(f"{tag}outd", r0)])
            if hT_out is not None:
                P.act(lambda e: e.activation(out=junk[:], in_=xt[:], func=AF.Square, accum_out=ss2[:, 0:1]),
                      reads=[f"{tag}xt"], writes=[f"{tag}junk", f"{tag}n_ss"])
                emit_rstd(P, tag + "n_", ss2, rs2)
                P.dve(lambda e, tt=tt: e.tensor_scalar(out=y[tt][:], in0=xt[:], scalar1=rs2[:, 1:2], scalar2=None, op0=ALU.mult),
                      reads=[f"{tag}xt", f"{tag}n_rs"], writes=yk)
        if hT_out is not None:
            emit_T_block(P, C, y, [[(f"{tag}y", tt, nb) for nb in range(8)] for tt in range(4)], gcol, f"{tag}gcol",
                         mx, mxk, psY[0:4], [(f"{tag}psY", i) for i in range(4)])
            P.dma("sp", f"{tag}hTo", lambda e, ps_=ps_: e.dma_start(out=hT_out[ps_], in_=mx[:]),
                  reads=mxk, writes=[(f"{tag}hTo", ps_)])
    return [f"{tag}out"] + ([f"{tag}hTo"] if hT_out is not None else [])


def build_outproj(TO, with_hT=False):
    nc = bass.Bass("TRN2", target_bir_lowering=False)
    mix_d = nc.dram_tensor("mix", [128, KC, TO], BF16, kind="ExternalInput").ap()
    wo_d = nc.dram_tensor("wo", [8, 2, 128, 16, 512], F32, kind="ExternalInput").ap()
    x_d = nc.dram_tensor("x", [TO, D_MODEL], F32, kind="ExternalInput").ap()
    g_d = nc.dram_tensor("gpost", [128, D_MODEL], F32, kind="ExternalInput").ap()
    out_d = nc.dram_tensor("out", [TO, D_MODEL], F32, kind="ExternalOutput").ap()
    gnext_d = hT_out = None
    if with_hT:
        gnext_d = nc.dram_tensor("gnext", [128, KC], F32, kind="ExternalInput").ap()
        hT_out = nc.dram_tensor("hTn", [TO // 512, 128, KC, 512], BF16, kind="ExternalOutput").ap()
    P = Prog(nc)
    C = make_consts(P)
    fin = emit_outproj(P, C, TO, mix_d, wo_d, x_d, g_d, out_d, gnext_d=gnext_d, hT_out=hT_out)
    P.finalize(final_wait_sems=fin)
    P.close()
    return nc


def lay_w_cols(w, col0):
    blk = w[:, col0:col0 + 128]
    return np.ascontiguousarray(blk.reshape(KC, 128, 128).transpose(1, 0, 2))


def lay_mixer_w(w_in, comp_offsets):
    NH = len(comp_offsets)
    out = np.empty((NH, 4, 128, KC, 128), np.float32)
    for h, offs in enumerate(comp_offsets):
        for c, o in enumerate(offs):
            out[h, c] = lay_w_cols(w_in, o)
    return out


def lay_wout(w_out):
    a = w_out.reshape(2, 16, 128, 8, 512)
    return np.ascontiguousarray(a.transpose(3, 0, 2, 1, 4))


def lay_bias(rel_bias_h):
    s = np.arange(128)[:, None, None]
    dl = np.arange(5)[None, :, None]
    t = np.arange(128)[None, None, :]
    rel = np.clip(t - s + 128 * dl, -128, 128) + 128
    return np.ascontiguousarray(rel_bias_h[rel].reshape(128, 5 * 128))


_CACHE = {}


def _get(name, fn):
    if name not in _CACHE:
        _CACHE[name] = fn()
    return _CACHE[name]


def lay_gcol(g):
    return np.ascontiguousarray(g.reshape(KC, 128).T).astype(np.float32)


def run_mixer_layer(layer, xin, norm_pre_l, w_in, b_f=None, rel_bias=None, hT=None):
    in_maps = []
    if layer == 0:
        types = "AAAABBBB"
        gcol = lay_gcol(norm_pre_l)
        WA = 16 * HD
        for b in range(BATCH):
            for g in range(4):
                offs = []
                for a in range(4):
                    h = 4 * g + a
                    offs.append([h * HD, WA + h * HD, 2 * WA + h * HD, 3 * WA + h * HD])
                for bb in range(4):
                    h = 4 * g + bb
                    offs.append([4 * WA + h * HD, 5 * WA + h * HD, 6 * WA + h * HD, 7 * WA + h * HD])
                waf = w_in[:, 8 * WA + 4 * g: 8 * WA + 4 * g + 4]
                waf = np.ascontiguousarray(waf.reshape(KC, 128, 4).transpose(1, 0, 2))
                in_maps.append({
                    "x": np.ascontiguousarray(xin[b]),
                    "gcol": gcol,
                    "w": lay_mixer_w(w_in, offs),
                    "waf": waf,
                    "bf": np.ascontiguousarray(b_f[4 * g:4 * g + 4].reshape(4, 1)),
                    "biasT": np.stack([lay_bias(rel_bias[4 * g + bb]) for bb in range(4)]),
                })
        nc = _get(("mixer", types), lambda: build_mixer(SEQ, types))
    else:
        types = "CCCCCCCC"
        WC = 32 * HD
        for b in range(BATCH):
            for g in range(4):
                offs = []
                for c in range(8):
                    h = 8 * g + c
                    offs.append([h * HD, WC + h * HD, 2 * WC + h * HD, 3 * WC + h * HD])
                in_maps.append({
                    "hT": np.ascontiguousarray(hT[b]),
                    "w": lay_mixer_w(w_in, offs),
                })
        nc = _get(("mixer", types), lambda: build_mixer(SEQ, types, ext_hT=True))
    res = run_bass_kernel_spmd(nc, in_maps, core_ids=list(range(8)))
    mixT = np.empty((BATCH, KC, 128, SEQ), dtype=ml_dtypes.bfloat16)
    for b in range(BATCH):
        for g in range(4):
            r = np.asarray(res.results[b * 4 + g]["mixT"])
            if layer == 0:
                mixT[b, 4 * g:4 * g + 4] = r[0:4]
                mixT[b, 16 + 4 * g:16 + 4 * g + 4] = r[4:8]
            else:
                mixT[b, 8 * g:8 * g + 8] = r
    return mixT


def run_outproj_layer(mixT, w_out, xin, norm_post_l, norm_pre_next=None):
    gpost = np.ascontiguousarray(np.broadcast_to(norm_post_l[None, :], (128, D_MODEL))).astype(np.float32)
    wo = lay_wout(w_out)
    TO = SEQ // 4
    with_hT = norm_pre_next is not None
    in_maps = []
    for b in range(BATCH):
        for r in range(4):
            m = {
                "mix": np.ascontiguousarray(mixT[b, :, :, r * TO:(r + 1) * TO].transpose(1, 0, 2)),
                "wo": wo,
                "x": np.ascontiguousarray(xin[b, r * TO:(r + 1) * TO]),
                "gpost": gpost,
            }
            if with_hT:
                m["gnext"] = lay_gcol(norm_pre_next)
            in_maps.append(m)
    nc = _get(("outproj", with_hT), lambda: build_outproj(TO, with_hT))
    res = run_bass_kernel_spmd(nc, in_maps, core_ids=list(range(8)))
    out = np.empty((BATCH, SEQ, D_MODEL), np.float32)
    hT = np.empty((BATCH, SEQ // 512, 128, KC, 512), dtype=ml_dtypes.bfloat16) if with_hT else None
    for b in range(BATCH):
        for r in range(4):
            out[b, r * TO:(r + 1) * TO] = np.asarray(res.results[b * 4 + r]["out"])
            if with_hT:
                hT[b, 2 * r:2 * r + 2] = np.asarray(res.results[b * 4 + r]["hTn"])
    return out, hT


def kernel(x, norm_pre, norm_post, w_in_even, b_f_even, rel_bias_even, w_out_even, w_in_odd, w_out_odd):
    x = np.asarray(x, np.float32)
    norm_pre = np.asarray(norm_pre, np.float32)
    norm_post = np.asarray(norm_post, np.float32)
    mix0 = run_mixer_layer(0, x, norm_pre[0], np.asarray(w_in_even[0], np.float32),
                           np.asarray(b_f_even[0], np.float32), np.asarray(rel_bias_even[0], np.float32))
    x1, h1T = run_outproj_layer(mix0, np.asarray(w_out_even[0], np.float32), x, norm_post[0], norm_pre_next=norm_pre[1])
    mix1 = run_mixer_layer(1, None, None, np.asarray(w_in_odd[0], np.float32), hT=h1T)
    out, _ = run_outproj_layer(mix1, np.asarray(w_out_odd[0], np.float32), x1, norm_post[1])
    return out
```

```python
import contextlib
import numpy as np
import ml_dtypes
import concourse.bass as bass
import concourse.mybir as mybir
from concourse.bass_utils import run_bass_kernel_spmd

F32 = mybir.dt.float32
BF16 = mybir.dt.bfloat16
I32 = mybir.dt.int32
AF = mybir.ActivationFunctionType
ALU = mybir.AluOpType

D_MODEL = 4096
SEQ = 4096
BATCH = 2
KC = D_MODEL // 128
HD = 128
EPS = 1e-6
NEG = -1.0e30

ENGS = ("pe", "act", "dve", "pool", "sp")


class Op:
    __slots__ = ("eng", "emit", "reads", "writes", "dma_sem", "deps", "signaled", "sig_val", "waits")

    def __init__(self, eng, emit, reads, writes, dma_sem):
        self.eng = eng
        self.emit = emit
        self.reads = reads
        self.writes = writes
        self.dma_sem = dma_sem
        self.deps = []
        self.signaled = False
        self.sig_val = 0
        self.waits = []


def _okey(o):
    return ("dma", o.dma_sem) if o.dma_sem is not None else ("eng", o.eng)


class Prog:
    def __init__(self, nc):
        self.nc = nc
        self.ops = {e: [] for e in ENGS}
        self.all_ops = []
        self.last_writer = {}
        self.readers = {}
        self.dma_sem_names = []
        self.dma_last = {}
        self.stack = contextlib.ExitStack()
        self.regs = {}

    def reg(self, eng, val):
        if val not in self.regs:
            self.regs[val] = eng.to_reg(val)
        return self.regs[val]

    def sb(self, name, shape, dt, stack=None):
        return (stack or self.stack).enter_context(self.nc.sbuf_tensor(name, shape, dt))

    def ps(self, name, shape, dt, stack=None):
        return (stack or self.stack).enter_context(self.nc.psum_tensor(name, shape, dt))

    def op(self, eng, emit, reads=(), writes=(), dma_sem=None):
        o = Op(eng, emit, tuple(reads), tuple(writes), dma_sem)
        deps = {}
        for k in o.reads:
            w = self.last_writer.get(k)
            if w is not None:
                deps[id(w)] = w
        for k in o.writes:
            w = self.last_writer.get(k)
            if w is not None:
                deps[id(w)] = w
            for r in self.readers.get(k, {}).values():
                deps[id(r)] = r
        if dma_sem is not None:
            if dma_sem not in self.dma_last:
                self.dma_sem_names.append(dma_sem)
            prev = self.dma_last.get(dma_sem)
            if prev is not None:
                deps[id(prev)] = prev
            self.dma_last[dma_sem] = o
        o.deps = [d for d in deps.values() if d is not o]
        for k in o.reads:
            self.readers.setdefault(k, {})[_okey(o)] = o
        for k in o.writes:
            self.last_writer[k] = o
            self.readers[k] = {}
        self.all_ops.append(o)
        self.ops[eng].append(o)
        return o

    def pe(self, emit, reads=(), writes=()):
        return self.op("pe", emit, reads, writes)

    def act(self, emit, reads=(), writes=()):
        return self.op("act", emit, reads, writes)

    def dve(self, emit, reads=(), writes=()):
        return self.op("dve", emit, reads, writes)

    def pool(self, emit, reads=(), writes=()):
        return self.op("pool", emit, reads, writes)

    def dma(self, eng, sem, emit, reads=(), writes=()):
        return self.op(eng, emit, reads, writes, dma_sem=sem)

    def barrier(self):
        lasts = []
        for e in ENGS:
            for o in reversed(self.ops[e]):
                if o.dma_sem is None:
                    lasts.append(o)
                    break
        lasts += list(self.dma_last.values())
        for e in ENGS:
            o = Op(e, lambda eng: eng.nop(), (), (), None)
            o.deps = [d for d in lasts]
            self.all_ops.append(o)
            self.ops[e].append(o)
        self.last_writer = {}
        self.readers = {}

    def finalize(self, final_wait_sems=()):
        nc = self.nc
        for o in self.all_ops:
            need = []
            for d in o.deps:
                if d.dma_sem is None and o.dma_sem is None and d.eng == o.eng:
                    if o.eng == "pe":
                        continue
                    if not any(k in d.writes for k in o.reads):
                        continue
                need.append(d)
            o.deps = need
            for d in need:
                if d.dma_sem is None:
                    d.signaled = True
        cnt = {e: 0 for e in ENGS}
        dcnt = {}
        for o in self.all_ops:
            if o.dma_sem is not None:
                dcnt[o.dma_sem] = dcnt.get(o.dma_sem, 0) + 16
                o.sig_val = dcnt[o.dma_sem]
            elif o.signaled:
                cnt[o.eng] += 1
                o.sig_val = cnt[o.eng]
        seen = {e: {} for e in ENGS}
        for o in self.all_ops:
            req = {}
            for d in o.deps:
                key = _okey(d)
                if d.sig_val > req.get(key, 0):
                    req[key] = d.sig_val
            s = seen[o.eng]
            for key, v in req.items():
                if s.get(key, 0) >= v:
                    continue
                s[key] = v
                o.waits.append((key, v))
        sems = {}
        for e in ENGS:
            sems[("eng", e)] = self.stack.enter_context(nc.semaphore("s_" + e))
        for n in self.dma_sem_names:
            sems[("dma", n)] = self.stack.enter_context(nc.semaphore("d_" + n))
        finals = [(("dma", n), dcnt[n]) for n in final_wait_sems]
        engmap = {"pe": "tensor", "act": "scalar", "dve": "vector", "pool": "gpsimd", "sp": "sync"}
        with nc.Block() as block:
            for e in ENGS:
                ops = self.ops[e]
                fin = finals if e == "sp" else []
                if not ops and not fin:
                    continue

                def body(eng, ops=ops, e=e, fin=fin):
                    for o in ops:
                        for key, v in o.waits:
                            eng.wait_ge(sems[key], v)
                        inst = o.emit(eng)
                        if o.dma_sem is not None:
                            inst.then_inc(sems[("dma", o.dma_sem)], 16)
                        elif o.signaled:
                            inst.then_inc(sems[("eng", e)], 1)
                    for key, v in fin:
                        eng.wait_ge(sems[key], v)

                getattr(block, engmap[e])(body)

    def close(self):
        self.stack.close()


def make_consts(P):
    C = {}
    idi = P.sb("c_idi", [128, 128], I32)
    C["idf"] = P.sb("c_idf", [128, 128], F32)
    C["idb"] = P.sb("c_idb", [128, 128], BF16)
    C["ones"] = P.sb("c_ones", [128, 128], BF16)
    C["tri"] = P.sb("c_tri", [128, 128], BF16)
    P.pool(lambda e: e.iota(idi[:], pattern=[[-1, 128]], base=0, channel_multiplier=1), writes=["c_idi"])
    P.dve(lambda e: e.tensor_scalar(out=C["idf"][:], in0=idi[:], scalar1=0, scalar2=None, op0=ALU.is_equal),
          reads=["c_idi"], writes=["c_idf"])
    P.dve(lambda e: e.tensor_copy(out=C["idb"][:], in_=C["idf"][:]), reads=["c_idf"], writes=["c_idb"])
    P.dve(lambda e: e.memset(C["ones"][:], 1.0), writes=["c_ones"])
    P.dve(lambda e: e.tensor_scalar(out=C["tri"][:], in0=idi[:], scalar1=0, scalar2=None, op0=ALU.is_ge),
          reads=["c_idi"], writes=["c_tri"])
    return C


def emit_T_block(P, C, hs, hs_keys, gcol, gkey, dst, dst_keys, pT, pT_keys):
    for kc in range(KC):
        pb = kc % len(pT)
        for j in range(4):
            P.pe(lambda e, kc=kc, j=j, pb=pb: e.transpose(out=pT[pb][:, j * 128:(j + 1) * 128],
                                                         in_=hs[j][:, kc * 128:(kc + 1) * 128], identity=C["idf"][:]),
                 reads=list(hs_keys[j]) + ["c_idf"], writes=[pT_keys[pb]])
        if kc % 2 == 0:
            P.act(lambda e, kc=kc, pb=pb: e.activation(out=dst[:, kc, :], in_=pT[pb][:], func=AF.Copy,
                                                       scale=gcol[:, kc:kc + 1]),
                  reads=[pT_keys[pb], gkey], writes=[dst_keys[kc]])
        else:
            P.dve(lambda e, kc=kc, pb=pb: e.tensor_scalar(out=dst[:, kc, :], in0=pT[pb][:], scalar1=gcol[:, kc:kc + 1],
                                                          scalar2=None, op0=ALU.mult),
                  reads=[pT_keys[pb], gkey], writes=[dst_keys[kc]])


def emit_rstd(P, ptag, ss, rs):
    P.dve(lambda e: e.tensor_scalar(out=ss[:, 1:2], in0=ss[:, 0:1], scalar1=1.0 / D_MODEL, scalar2=EPS,
                                    op0=ALU.mult, op1=ALU.add), reads=[f"{ptag}ss"], writes=[f"{ptag}ss2"])
    P.act(lambda e: e.activation(out=rs[:, 0:1], in_=ss[:, 1:2], func=AF.Sqrt),
          reads=[f"{ptag}ss2"], writes=[f"{ptag}rs0"])
    P.dve(lambda e: e.reciprocal(out=rs[:, 1:2], in_=rs[:, 0:1]), reads=[f"{ptag}rs0"], writes=[f"{ptag}rs"])


def emit_rmsnorm_T(P, C, x_d, gcol_d, n_tt, hTb, sink, stk, ptag):
    xt = [P.sb(f"{ptag}xt{i}", [128, D_MODEL], F32, stk) for i in range(2)]
    hs = [P.sb(f"{ptag}hs{i}", [128, D_MODEL], F32, stk) for i in range(4)]
    junk = P.sb(f"{ptag}junk", [128, D_MODEL], BF16, stk)
    gcol = P.sb(f"{ptag}gcol", [128, KC], F32, stk)
    ss = P.sb(f"{ptag}ss", [128, 2], F32, stk)
    rs = P.sb(f"{ptag}rs", [128, 2], F32, stk)
    pT = [P.ps(f"{ptag}pT{i}", [128, 512], F32, stk) for i in range(4)]
    P.dma("sp", f"{ptag}gcol", lambda e: e.dma_start(out=gcol[:], in_=gcol_d), writes=[f"{ptag}gcol"])
    for tt in range(n_tt):
        tb, j = divmod(tt, 4)
        slot = tb % 2
        xs = tt % 2
        P.dma("sp", f"{ptag}xt{xs}",
              lambda e, tt=tt, xs=xs: e.dma_start(out=xt[xs][:], in_=x_d[tt * 128:(tt + 1) * 128, :]),
              writes=[f"{ptag}xt{xs}"])
        P.act(lambda e, xs=xs: e.activation(out=junk[:], in_=xt[xs][:], func=AF.Square, accum_out=ss[:, 0:1]),
              reads=[f"{ptag}xt{xs}"], writes=[f"{ptag}junk", f"{ptag}ss"])
        emit_rstd(P, ptag, ss, rs)
        P.dve(lambda e, xs=xs, j=j: e.tensor_scalar(out=hs[j][:], in0=xt[xs][:], scalar1=rs[:, 1:2], scalar2=None,
                                                    op0=ALU.mult),
              reads=[f"{ptag}xt{xs}", f"{ptag}rs"], writes=[(f"{ptag}hs", j)])
        if j == 3:
            keys = [("hTbk", slot, kc) for kc in range(KC)]
            emit_T_block(P, C, hs, [[(f"{ptag}hs", jj)] for jj in range(4)], gcol, f"{ptag}gcol",
                         hTb[slot], keys, pT, [f"{ptag}pT{i}" for i in range(4)])
            sink(tb, slot, keys)


def emit_mixer(P, C, T, head_types, x_d, g_d, w_d, mix_d, hT_d, waf_d=None, bf_d=None, bias_d=None, tag="m",
                phase1=True, overlap=True):
    NH = len(head_types)
    NTB = T // 512
    NKT = T // 128
    NA = sum(1 for h in head_types if h == "A")
    has_A = NA > 0
    has_B = "B" in head_types
    has_C = "C" in head_types
    scale = HD ** -0.5

    hTb = [P.sb(f"{tag}hTb{i}", [128, KC, 512], BF16) for i in range(2)]

    if phase1:
        stk1 = contextlib.ExitStack()

        def sink(tb, slot, keys):
            P.dma("pool", f"{tag}hTs{slot}", lambda e: e.dma_start(out=hT_d[tb], in_=hTb[slot][:]),
                  reads=keys, writes=[("hTd", tb)])

        emit_rmsnorm_T(P, C, x_d, g_d, NKT, hTb, sink, stk1, tag + "1")
        P.barrier()
        stk1.close()

    wb = [P.sb(f"{tag}wb{c}", [128, KC, 128], BF16) for c in range(4)]
    NSET = 2 if overlap else 1
    qTs = [P.sb(f"{tag}qT{i}", [128, T], BF16) for i in range(NSET)]
    kTs = [P.sb(f"{tag}kT{i}", [128, T], BF16) for i in range(NSET)]
    sgTs = [P.sb(f"{tag}sgT{i}", [128, T], BF16) for i in range(NSET)]
    vtoks = [P.sb(f"{tag}vtok{i}", [128, NKT, 128], BF16) for i in range(NSET)]
    vTb = P.sb(f"{tag}vTb", [128, 512], BF16)
    ge = P.sb(f"{tag}ge", [128, 512], F32)
    mixs = [P.sb(f"{tag}mixs{i}", [128, 512], BF16) for i in range(2)]
    psP = [P.ps(f"{tag}psP{i}", [128, 512], F32) for i in range(2)]
    psT = P.ps(f"{tag}psT", [128, 4, 128], BF16)
    psZ = [P.ps(f"{tag}psZ{i}", [128, 512], F32) for i in range(2)]
    psX = [P.ps(f"{tag}psX{i}", [128, 512], F32) for i in range(2)]
    psO = P.ps(f"{tag}psO", [128, 512], F32)
    if has_A or has_B:
        lg = [P.sb(f"{tag}lg{i}", [128, 512], F32) for i in range(2)]
        Pt = [P.sb(f"{tag}Pt{i}", [128, 512], BF16) for i in range(2)]
        rden = P.sb(f"{tag}rden", [128, 512], F32)
        onrm = P.sb(f"{tag}onrm", [128, 512], F32)
    if has_A:
        waf = P.sb(f"{tag}waf", [128, KC, NA], BF16)
        bfc = P.sb(f"{tag}bfc", [NA, 2], F32)
        e1 = P.sb(f"{tag}e1", [NA, 512], F32)
        spb = P.sb(f"{tag}spb", [NA, 512], F32)
        csp = P.sb(f"{tag}csp", [NA, T], F32)
        seli = P.sb(f"{tag}seli", [NA, 128], I32)
        nsel = [P.sb(f"{tag}nsel{a}", [NA, 128], F32) for a in range(NA)]
        cpos = P.sb(f"{tag}cpos", [128, NKT, NA], F32)
        Cb = [P.sb(f"{tag}Cb{i}", [128, 512], F32) for i in range(2)]
        P.dma("pool", f"{tag}waf", lambda e: e.dma_start(out=waf[:], in_=waf_d), writes=["waf"])
        P.dma("sp", f"{tag}bfc", lambda e: e.dma_start(out=bfc[:, 0:1], in_=bf_d), writes=["bfc0"])
        P.dve(lambda e: e.tensor_scalar(out=bfc[:, 1:2], in0=bfc[:, 0:1], scalar1=-1.0, scalar2=None, op0=ALU.mult),
              reads=["bfc0"], writes=["bfc"])
        for a in range(NA):
            P.pool(lambda e, a=a: e.iota(seli[:], pattern=[[0, 128]], base=-a, channel_multiplier=1),
                   writes=["seli"])
            P.dve(lambda e, a=a: e.tensor_scalar(out=nsel[a][:], in0=seli[:], scalar1=0, scalar2=-1.0,
                                                 op0=ALU.is_equal, op1=ALU.mult),
                  reads=["seli"], writes=[("nsel", a)])
    if has_B:
        biasT = [P.sb(f"{tag}biasT{i}", [128, 5 * 128], F32) for i in range(1)]
    if has_C:
        Et = [P.sb(f"{tag}E{i}", [128, 512], F32) for i in range(4)]
        Lt = [P.sb(f"{tag}L{i}", [128, 512], BF16) for i in range(2)]
        Dm = [P.sb(f"{tag}Dm{i}", [128, 512], F32) for i in range(2)]
        At = [P.sb(f"{tag}At{i}", [128, 512], BF16) for i in range(2)]
        Acc = P.sb(f"{tag}Acc", [128, 512], BF16)

    cnt = {"blk": 0, "z": 0, "s": 0, "mix": 0, "r": 0}

    def load_w(hs):
        for c in range(4):
            P.dma("pool", f"{tag}wb{c}", lambda e, c=c: e.dma_start(out=wb[c][:], in_=w_d[hs, c]),
                  writes=[("wb", c)])

    def load_hT(tb):
        slot = cnt["blk"] % 2
        cnt["blk"] += 1
        P.dma("sp", f"{tag}hTl{slot}", lambda e: e.dma_start(out=hTb[slot][:], in_=hT_d[tb]),
              reads=[("hTd", tb)], writes=[("hTb", slot)])
        return slot

    def mix_out(hs, q0, ncols, emit_mix, reads):
        ms = cnt["mix"] % 2
        cnt["mix"] += 1
        emit_mix(mixs[ms], ms, reads)
        P.dma("sp", f"{tag}mixo{ms}", lambda e: e.dma_start(out=mix_d[hs, :, q0:q0 + ncols], in_=mixs[ms][:, 0:ncols]),
              reads=[("mixs", ms)], writes=[("mixd", hs, q0)])


    if overlap:
        obanks = [(psO, "psO")]
        dbanks = [(psX[0], ("psX", 0))]
        rbanks = [(psX[0], ("psX", 0)), (psX[1], ("psX", 1))]
    else:
        obanks = [(psO, "psO"), (psP[0], ("psP", 0))]
        dbanks = [(psX[0], ("psX", 0)), (psP[1], ("psP", 1))]
        rbanks = [(psX[0], ("psX", 0)), (psX[1], ("psX", 1)), (psP[1], ("psP", 1))]
    NOB = len(obanks)
    NRB = len(rbanks)
    cur = {"set": 0, "filler": None}
    st = {"n": 0, "q": 0}

    def finish_q(hs, t, ob, db, normalize):
        q0, qi = t["q0"], t["qi"]
        sgT = sgTs[cur["set"]]
        cs_ = cur["set"]
        ms = cnt["mix"] % 2
        cnt["mix"] += 1
        if normalize:
            P.dve(lambda e: e.reciprocal(out=rden[:], in_=db[0][:]), reads=[db[1]], writes=["rden"])
            P.dve(lambda e: e.tensor_tensor(out=onrm[:], in0=ob[0][:], in1=rden[:], op=ALU.mult),
                  reads=[ob[1], "rden"], writes=["onrm"])
            P.dve(lambda e: e.tensor_tensor(out=mixs[ms][:], in0=onrm[:], in1=sgT[:, q0:q0 + 512], op=ALU.mult),
                  reads=["onrm", ("sg", cs_, qi)], writes=[("mixs", ms)])
        else:
            P.dve(lambda e: e.tensor_tensor(out=mixs[ms][:], in0=ob[0][:], in1=sgT[:, q0:q0 + 512], op=ALU.mult),
                  reads=[ob[1], ("sg", cs_, qi)], writes=[("mixs", ms)])
        P.dma("pool", f"{tag}mixo{ms}", lambda e: e.dma_start(out=mix_d[hs, :, q0:q0 + 512], in_=mixs[ms][:]),
              reads=[("mixs", ms)], writes=[("mixd", hs, q0)])

    def stream_AB(hs, tiles):
        qT, kT, sgT, vtok = qTs[cur['set']], kTs[cur['set']], sgTs[cur['set']], vtoks[cur['set']]
        cs_ = cur['set']
        N = len(tiles)
        base = st["n"]
        st["n"] += N
        for t in tiles:
            if t["first"]:
                st["q"] += 1
            t["ob"] = obanks[st["q"] % NOB]
            t["db"] = dbanks[st["q"] % NOB]
            t["cbs"] = st["q"] % 2

        def s_z(i):
            t = tiles[i]
            n = base + i
            zb = n % 2
            kt, c0, c1, q0 = t["kt"], t["c0"], t["c1"], t["q0"]
            if t["kind"] == "A" and t["first"]:
                a, cbs, qi = t["a"], t["cbs"], t["qi"]
                P.pe(lambda e: e.matmul(psX[1][:], lhsT=nsel[a][:], rhs=csp[:, q0:q0 + 512], start=True, stop=True),
                     reads=[("nsel", a), ("csp", qi)], writes=[("psX", 1)])
                P.act(lambda e: e.activation(out=Cb[cbs][:], in_=psX[1][:], func=AF.Copy),
                      reads=[("psX", 1)], writes=[("Cb", cbs)])
            P.pe(lambda e: e.matmul(psZ[zb][:, c0:c1], lhsT=kT[:, kt * 128:(kt + 1) * 128], rhs=qT[:, q0 + c0:q0 + c1],
                                    start=True, stop=True),
                 reads=[("kT", cs_, kt // 4), ("qT", cs_, t["qi"])], writes=[("psZ", zb)])

        def s_lg(i):
            t = tiles[i]
            n = base + i
            zb = n % 2
            sb_ = n % 2
            kt, c0, c1 = t["kt"], t["c0"], t["c1"]
            if t["kind"] == "A":
                a, cbs = t["a"], t["cbs"]
                P.dve(lambda e: e.scalar_tensor_tensor(out=lg[sb_][:, c0:c1], in0=psZ[zb][:, c0:c1],
                                                       scalar=cpos[:, kt, a:a + 1], in1=Cb[cbs][:, c0:c1],
                                                       op0=ALU.add, op1=ALU.add),
                      reads=[("psZ", zb), "cpos", ("Cb", cbs)], writes=[("lg", sb_)])
            else:
                bs, dlo = t["bs"], t["dlo"]
                P.dve(lambda e: e.tensor_tensor(out=lg[sb_][:, c0:c1], in0=psZ[zb][:, c0:c1],
                                                in1=biasT[bs][:, dlo * 128:dlo * 128 + (c1 - c0)], op=ALU.add),
                      reads=[("psZ", zb), ("biasT", bs)], writes=[("lg", sb_)])
            for mk in t["masks"]:
                if mk[0] == "tri":
                    cc = mk[1]
                    P.pool(lambda e, cc=cc: e.affine_select(
                        out=lg[sb_][:, cc:cc + 128], in_=lg[sb_][:, cc:cc + 128], pattern=[[1, 128]],
                        compare_op=ALU.is_ge, fill=P.reg(e, NEG), base=0, channel_multiplier=-1),
                        reads=[("lg", sb_)], writes=[("lg", sb_)])
                else:
                    _, p0, p1, x0, x1 = mk
                    P.pool(lambda e, p0=p0, p1=p1, x0=x0, x1=x1: e.memset(lg[sb_][p0:p1, x0:x1], NEG),
                           reads=[("lg", sb_)], writes=[("lg", sb_)])

        def s_p(i):
            t = tiles[i]
            n = base + i
            sb_ = n % 2
            pb = n % 2
            c0, c1 = t["c0"], t["c1"]
            P.act(lambda e: e.activation(out=Pt[pb][:, c0:c1], in_=lg[sb_][:, c0:c1], func=AF.Exp),
                  reads=[("lg", sb_)], writes=[("Pt", pb)])

        def s_pv(i):
            t = tiles[i]
            n = base + i
            pb = n % 2
            kt, c0, c1 = t["kt"], t["c0"], t["c1"]
            ob, db = t["ob"], t["db"]
            P.pe(lambda e: e.matmul(ob[0][:, c0:c1], lhsT=vtok[:, kt, :], rhs=Pt[pb][:, c0:c1],
                                    start=t["first"], stop=t["last"]),
                 reads=[("v", cs_, kt // 4), ("Pt", pb)], writes=[ob[1]])
            P.pe(lambda e: e.matmul(db[0][:, c0:c1], lhsT=C["ones"][:], rhs=Pt[pb][:, c0:c1],
                                    start=t["first"], stop=t["last"]),
                 reads=["c_ones", ("Pt", pb)], writes=[db[1]])
            if t["last"]:
                finish_q(hs, t, ob, db, True)

        for i in range(-3, N):
            if 0 <= i + 3 < N:
                s_z(i + 3)
            if 0 <= i + 2 < N:
                s_lg(i + 2)
            if 0 <= i + 1 < N:
                s_p(i + 1)
            if cur["filler"] is not None:
                cur["filler"](N)
            if 0 <= i:
                s_pv(i)

    def stream_C(hs, tiles):
        qT, kT, sgT, vtok = qTs[cur['set']], kTs[cur['set']], sgTs[cur['set']], vtoks[cur['set']]
        cs_ = cur['set']
        N = len(tiles)
        base = st["n"]
        st["n"] += N
        for t in tiles:
            if t["first"]:
                st["q"] += 1
            t["ob"] = obanks[st["q"] % NOB]

        def s_z(i):
            t = tiles[i]
            zb = (base + i) % 2
            kt, c0, q0 = t["kt"], t["c0"], t["q0"]
            P.pe(lambda e: e.matmul(psZ[zb][:, c0:512], lhsT=kT[:, kt * 128:(kt + 1) * 128], rhs=qT[:, q0 + c0:q0 + 512],
                                    start=True, stop=True),
                 reads=[("kT", cs_, kt // 4), ("qT", cs_, t["qi"])], writes=[("psZ", zb)])

        def s_E(i):
            t = tiles[i]
            n = base + i
            c0 = t["c0"]
            P.act(lambda e: e.activation(out=Et[n % 4][:, c0:512], in_=psZ[n % 2][:, c0:512], func=AF.Exp),
                  reads=[("psZ", n % 2)], writes=[("E", n % 4)])

        def s_L(i):
            t = tiles[i]
            n = base + i
            c0 = t["c0"]
            ls = n % 2
            rbk, rkey = rbanks[n % NRB]
            P.act(lambda e: e.activation(out=Lt[ls][:, c0:512], in_=Et[n % 4][:, c0:512], func=AF.Ln, bias=1.0),
                  reads=[("E", n % 4)], writes=[("L", ls)])
            if t["diag"]:
                P.pool(lambda e: e.affine_select(
                    out=Lt[ls][:, c0:c0 + 128], in_=Lt[ls][:, c0:c0 + 128], pattern=[[1, 128]],
                    compare_op=ALU.is_gt, fill=P.reg(e, 0.0), base=0, channel_multiplier=-1),
                    reads=[("L", ls)], writes=[("L", ls)])
            if t["first"]:
                P.pool(lambda e: e.memset(Acc[:], 0.0), writes=["Acc"])

        def s_R(i):
            t = tiles[i]
            n = base + i
            c0 = t["c0"]
            ls = n % 2
            rbk, rkey = rbanks[n % NRB]
            P.pe(lambda e: e.matmul(rbk[:, c0:512], lhsT=C["tri"][:], rhs=Lt[ls][:, c0:512], start=True, stop=t["first"]),
                 reads=["c_tri", ("L", ls)], writes=[rkey])
            if not t["first"]:
                P.pe(lambda e: e.matmul(rbk[:, c0:512], lhsT=C["ones"][:], rhs=Acc[:, c0:512], start=False, stop=True),
                     reads=["c_ones", "Acc"], writes=[rkey])
            if not t["last"]:
                P.dve(lambda e: e.tensor_tensor(out=Acc[:, c0:512], in0=Acc[:, c0:512], in1=Lt[ls][:, c0:512], op=ALU.add),
                      reads=["Acc", ("L", ls)], writes=["Acc"])

        def s_D(i):
            t = tiles[i]
            n = base + i
            c0 = t["c0"]
            ds_ = n % 2
            rbk, rkey = rbanks[n % NRB]
            P.act(lambda e: e.activation(out=Dm[ds_][:, c0:512], in_=rbk[:, c0:512], func=AF.Exp, scale=-1.0),
                  reads=[rkey], writes=[("Dm", ds_)])
            P.dve(lambda e: e.tensor_tensor(out=At[ds_][:, c0:512], in0=Et[n % 4][:, c0:512], in1=Dm[ds_][:, c0:512], op=ALU.mult),
                  reads=[("E", n % 4), ("Dm", ds_)], writes=[("At", ds_)])
            if t["diag"]:
                P.pool(lambda e: e.affine_select(
                    out=At[ds_][:, c0:c0 + 128], in_=At[ds_][:, c0:c0 + 128], pattern=[[1, 128]],
                    compare_op=ALU.is_gt, fill=P.reg(e, 0.0), base=0, channel_multiplier=-1),
                    reads=[("At", ds_)], writes=[("At", ds_)])

        def s_PV(i):
            t = tiles[i]
            n = base + i
            c0, kt = t["c0"], t["kt"]
            ds_ = n % 2
            ob = t["ob"]
            P.pe(lambda e: e.matmul(ob[0][:, c0:512], lhsT=vtok[:, kt, :], rhs=At[ds_][:, c0:512], start=t["first"], stop=t["last"]),
                 reads=[("v", cs_, kt // 4), ("At", ds_)], writes=[ob[1]])
            if t["last"]:
                finish_q(hs, t, ob, None, False)

        for i in range(-3, N):
            if 0 <= i + 3 < N:
                s_z(i + 3)
            if 0 <= i + 2 < N:
                s_E(i + 2)
            if 0 <= i:
                s_D(i)
            if 0 <= i + 2 < N:
                s_L(i + 2)
            if cur["filler"] is not None:
                cur["filler"](N)
            if 0 <= i + 2 < N:
                s_R(i + 2)
            if 0 <= i:
                s_PV(i)

    def silu_parts(pb, dst, dkey):
        def p1():
            P.act(lambda e: e.activation(out=ge[:], in_=psP[pb][:], func=AF.Exp, scale=-1.0),
                  reads=[("psP", pb)], writes=["ge"])
        def p2():
            P.dve(lambda e: e.tensor_scalar(out=ge[:], in0=ge[:], scalar1=1.0, scalar2=None, op0=ALU.add),
                  reads=["ge"], writes=["ge"])
        def p3():
            P.dve(lambda e: e.reciprocal(out=ge[:], in_=ge[:]), reads=["ge"], writes=["ge"])
            P.dve(lambda e: e.tensor_tensor(out=dst, in0=psP[pb][:], in1=ge[:], op=ALU.mult),
                  reads=[("psP", pb), "ge"], writes=[dkey])
        return [p1, p2, p3]

    def proj_gen(hs, sset):
        qT, kT, sgT, vtok = qTs[sset], kTs[sset], sgTs[sset], vtoks[sset]
        with_af = has_A and hs == 0
        pend = []

        def defer(n, fn):
            pend.append([n, fn])

        def step():
            for it in list(pend):
                it[0] -= 1
                if it[0] <= 0:
                    it[1]()
                    pend.remove(it)

        slot = load_hT(0)
        for tb in range(NTB):
            nslot = load_hT(tb + 1) if tb + 1 < NTB else None
            cs = slice(tb * 512, (tb + 1) * 512)
            for c in range(4):
                pb = c % 2
                for kc in range(KC):
                    P.pe(lambda e, c=c, kc=kc, pb=pb, slot=slot: e.matmul(
                        psP[pb][:], lhsT=wb[c][:, kc, :], rhs=hTb[slot][:, kc, :], start=(kc == 0), stop=(kc == KC - 1)),
                        reads=[("wb", c), ("hTb", slot)], writes=[("psP", pb)])
                    if kc % 8 == 7:
                        if kc == KC - 1:
                            if c == 0:
                                defer(1, lambda pb=pb, cs=cs, tb=tb: P.act(
                                    lambda e: e.activation(out=qT[:, cs], in_=psP[pb][:], func=AF.Copy, scale=scale),
                                    reads=[("psP", pb)], writes=[("qT", sset, tb)]))
                            elif c == 1:
                                defer(1, lambda pb=pb, cs=cs, tb=tb: P.dve(
                                    lambda e: e.tensor_copy(out=kT[:, cs], in_=psP[pb][:]),
                                    reads=[("psP", pb)], writes=[("kT", sset, tb)]))
                            elif c == 2:
                                defer(1, lambda pb=pb: P.dve(lambda e: e.tensor_copy(out=vTb[:], in_=psP[pb][:]),
                                                             reads=[("psP", pb)], writes=["vTb"]))

                                def vtr():
                                    for i in range(4):
                                        P.pe(lambda e, i=i: e.transpose(out=psT[:, i, :], in_=vTb[:, i * 128:(i + 1) * 128],
                                                                        identity=C["idb"][:]),
                                             reads=["vTb", "c_idb"], writes=["psT"])
                                defer(2, vtr)
                                defer(3, lambda tb=tb: P.act(
                                    lambda e: e.activation(out=vtok[:, tb * 4:(tb + 1) * 4, :], in_=psT[:], func=AF.Copy),
                                    reads=["psT"], writes=[("v", sset, tb)]))
                            else:
                                parts = silu_parts(pb, sgT[:, cs], ("sg", sset, tb))
                                defer(1, parts[0])
                                defer(2, parts[1])
                                defer(3, parts[2])
                        yield
                        step()
            if with_af:
                for kc in range(KC):
                    P.pe(lambda e, kc=kc, slot=slot: e.matmul(
                        psO[0:NA, :], lhsT=waf[:, kc, :], rhs=hTb[slot][:, kc, :], start=(kc == 0), stop=(kc == KC - 1)),
                        reads=["waf", ("hTb", slot)], writes=["psO"])
                P.act(lambda e: e.activation(out=e1[:], in_=psO[0:NA, :], func=AF.Exp, scale=-1.0, bias=bfc[:, 1:2]),
                      reads=["psO", "bfc"], writes=["e1"])
                P.act(lambda e: e.activation(out=spb[:], in_=e1[:], func=AF.Ln, bias=1.0),
                      reads=["e1"], writes=["spb"])
                if tb == 0:
                    P.dve(lambda e, cs=cs: e.tensor_tensor_scan(out=csp[:, cs], data0=spb[:], data1=spb[:], initial=0.0,
                                                                op0=ALU.add, op1=ALU.bypass),
                          reads=["spb"], writes=[("csp", tb)])
                else:
                    P.dve(lambda e, cs=cs, tb=tb: e.tensor_tensor_scan(out=csp[:, cs], data0=spb[:], data1=spb[:],
                                                                       initial=csp[:, tb * 512 - 1:tb * 512],
                                                                       op0=ALU.add, op1=ALU.bypass),
                          reads=["spb", ("csp", tb - 1)], writes=[("csp", tb)])
            slot = nslot
        while pend:
            step()
        if with_af:
            for kt in range(NKT):
                P.pe(lambda e, kt=kt: e.matmul(psX[0][:, kt * NA:(kt + 1) * NA], lhsT=csp[:, kt * 128:(kt + 1) * 128],
                                               rhs=C["idf"][0:NA, 0:NA], start=True, stop=True),
                     reads=[("csp", kt // 4), "c_idf"], writes=[("psX", 0)])
            P.dve(lambda e: e.tensor_copy(out=cpos[:].rearrange("p k a -> p (k a)"), in_=psX[0][:, 0:NKT * NA]),
                  reads=[("psX", 0)], writes=["cpos"])

    PROJ_STEPS = NTB * 16
    a_idx = 0
    b_idx = 0
    load_w(0)
    for _ in proj_gen(0, 0):
        pass
    for hs, ht in enumerate(head_types):
        cur["set"] = hs % NSET
        gen = None
        if hs + 1 < NH:
            load_w(hs + 1)
            gen = proj_gen(hs + 1, (hs + 1) % NSET)
        if overlap and gen is not None:
            state = {"done": 0, "it": 0, "alive": True}

            def filler(n_iters, state=state, gen=gen):
                state["it"] += 1
                want = min(PROJ_STEPS, (state["it"] * PROJ_STEPS + n_iters - 1) // n_iters)
                while state["alive"] and state["done"] < want:
                    try:
                        next(gen)
                        state["done"] += 1
                    except StopIteration:
                        state["alive"] = False
            cur["filler"] = filler
        else:
            cur["filler"] = None
        if ht == "A":
            a = a_idx
            a_idx += 1
            tiles = []
            for qi in range(NTB):
                nk = 4 * qi + 4
                for kt in range(nk):
                    m = kt - 4 * qi
                    c0 = 128 * max(m, 0)
                    tiles.append(dict(kind="A", a=a, qi=qi, kt=kt, c0=c0, c1=512, q0=qi * 512,
                                      masks=([("tri", c0)] if m >= 0 else []),
                                      first=(kt == 0), last=(kt == nk - 1)))
            stream_AB(hs, tiles)
        elif ht == "B":
            b = b_idx
            b_idx += 1
            bs = 0
            P.dma("sp", f"{tag}biasT{bs}", lambda e, b=b, bs=bs: e.dma_start(out=biasT[bs][:], in_=bias_d[b]),
                  writes=[("biasT", bs)])
            tiles = []
            for qg in range(NTB):
                jf = max(0, 4 * qg - 4)
                for j in range(jf, 4 * qg + 4):
                    m_lo = max(j, 4 * qg)
                    m_hi = min(j + 4, 4 * qg + 3)
                    c0 = (m_lo - 4 * qg) * 128
                    c1 = (m_hi - 4 * qg + 1) * 128
                    masks = []
                    for m in range(m_lo, m_hi + 1):
                        col = (m - 4 * qg) * 128
                        if m - j == 0:
                            masks.append(("blk", 64, 128, col, col + 64))
                        if m - j == 4:
                            masks.append(("blk", 0, 64, col + 64, col + 128))
                    tiles.append(dict(kind="B", bs=bs, dlo=m_lo - j, qi=qg, kt=j, c0=c0, c1=c1, q0=qg * 512,
                                      masks=masks, first=(j == jf), last=(j == 4 * qg + 3)))
            stream_AB(hs, tiles)
        else:
            tiles = []
            for qi in range(NTB):
                nk = 4 * qi + 4
                for kt in range(nk - 1, -1, -1):
                    m = kt - 4 * qi
                    c0 = 128 * max(m, 0)
                    tiles.append(dict(qi=qi, kt=kt, c0=c0, c1=512, q0=qi * 512, diag=(m >= 0),
                                      first=(kt == nk - 1), last=(kt == 0)))
            stream_C(hs, tiles)

        cur["filler"] = None
        if gen is not None:
            for _ in gen:
                pass
    return [f"{tag}mixo0", f"{tag}mixo1"]


def build_mixer(T, head_types, ext_hT=False):
    nc = bass.Bass("TRN2", target_bir_lowering=False)
    NH = len(head_types)
    NA = sum(1 for h in head_types if h == "A")
    NB = sum(1 for h in head_types if h == "B")
    x_d = g_d = None
    if ext_hT:
        hT_d = nc.dram_tensor("hT", [T // 512, 128, KC, 512], BF16, kind="ExternalInput").ap()
    else:
        x_d = nc.dram_tensor("x", [T, D_MODEL], F32, kind="ExternalInput").ap()
        g_d = nc.dram_tensor("gcol", [128, KC], F32, kind="ExternalInput").ap()
        hT_d = nc.dram_tensor("hT_scr", [T // 512, 128, KC, 512], BF16, kind="Internal").ap()
    w_d = nc.dram_tensor("w", [NH, 4, 128, KC, 128], F32, kind="ExternalInput").ap()
    mix_d = nc.dram_tensor("mixT", [NH, 128, T], BF16, kind="ExternalOutput").ap()
    waf_d = bf_d = bias_d = None
    if NA:
        waf_d = nc.dram_tensor("waf", [128, KC, NA], F32, kind="ExternalInput").ap()
        bf_d = nc.dram_tensor("bf", [NA, 1], F32, kind="ExternalInput").ap()
    if NB:
        bias_d = nc.dram_tensor("biasT", [NB, 128, 5 * 128], F32, kind="ExternalInput").ap()
    P = Prog(nc)
    C = make_consts(P)
    fin = emit_mixer(P, C, T, head_types, x_d, g_d, w_d, mix_d, hT_d, waf_d, bf_d, bias_d, phase1=not ext_hT)
    P.finalize(final_wait_sems=fin)
    P.close()
    return nc


def emit_outproj(P, C, TO, mix_d, wo_d, x_d, g_d, out_d, tag="o", gnext_d=None, hT_out=None):
    NPASS = TO // 512
    mx = P.sb(f"{tag}mx", [128, KC, 512], BF16)
    wsl = [P.sb(f"{tag}wsl{i}", [128, 16, 512], BF16) for i in range(2)]
    y = [P.sb(f"{tag}y{i}", [128, D_MODEL], F32) for i in range(4)]
    gb = P.sb(f"{tag}gb", [128, D_MODEL], F32)
    xt = P.sb(f"{tag}xt", [128, D_MODEL], F32)
    junk = P.sb(f"{tag}junk", [128, D_MODEL], BF16)
    ss = P.sb(f"{tag}ss", [128, 2], F32)
    rs = P.sb(f"{tag}rs", [128, 2], F32)
    psY = [P.ps(f"{tag}psY{i}", [128, 512], F32) for i in range(8)]
    P.dma("sp", f"{tag}gb", lambda e: e.dma_start(out=gb[:], in_=g_d[:, :]), writes=[f"{tag}gb"])
    mxk = [(f"{tag}mx", kc) for kc in range(KC)]
    if hT_out is not None:
        gcol = P.sb(f"{tag}gcol", [128, KC], F32)
        ss2 = P.sb(f"{tag}n_ss", [128, 2], F32)
        rs2 = P.sb(f"{tag}n_rs", [128, 2], F32)
        P.dma("sp", f"{tag}gcol", lambda e: e.dma_start(out=gcol[:], in_=gnext_d), writes=[f"{tag}gcol"])
    wcnt = 0
    ev = 0
    for ps_ in range(NPASS):
        t0 = ps_ * 512
        P.dma("sp", f"{tag}mx", lambda e, t0=t0: e.dma_start(out=mx[:], in_=mix_d[:, :, t0:t0 + 512]),
              writes=mxk)
        for nb in range(8):
            for kh in range(2):
                ws = wcnt % 2
                wcnt += 1
                P.dma("pool", f"{tag}wsl{ws}", lambda e, nb=nb, kh=kh, ws=ws: e.dma_start(out=wsl[ws][:], in_=wo_d[nb, kh]),
                      writes=[(f"{tag}wsl", ws)])
                for tt in range(4):
                    pb = (nb % 2) * 4 + tt
                    for kc in range(16):
                        P.pe(lambda e, tt=tt, kc=kc, kh=kh, ws=ws, pb=pb: e.matmul(
                            psY[pb][:], lhsT=mx[:, kh * 16 + kc, tt * 128:(tt + 1) * 128], rhs=wsl[ws][:, kc, :],
                            start=(kh == 0 and kc == 0), stop=(kh == 1 and kc == 15)),
                            reads=[mxk[kh * 16 + kc], (f"{tag}wsl", ws)], writes=[(f"{tag}psY", pb)])
            for tt in range(4):
                pb = (nb % 2) * 4 + tt
                if ev % 2 == 0:
                    P.act(lambda e, tt=tt, nb=nb, pb=pb: e.activation(out=y[tt][:, nb * 512:(nb + 1) * 512], in_=psY[pb][:], func=AF.Copy),
                          reads=[(f"{tag}psY", pb)], writes=[(f"{tag}y", tt, nb)])
                else:
                    P.dve(lambda e, tt=tt, nb=nb, pb=pb: e.tensor_copy(out=y[tt][:, nb * 512:(nb + 1) * 512], in_=psY[pb][:]),
                          reads=[(f"{tag}psY", pb)], writes=[(f"{tag}y", tt, nb)])
                ev += 1
        for tt in range(4):
            r0 = t0 + tt * 128
            yk = [(f"{tag}y", tt, nb) for nb in range(8)]
            P.dma("sp", f"{tag}xt", lambda e, r0=r0: e.dma_start(out=xt[:], in_=x_d[r0:r0 + 128, :]), writes=[f"{tag}xt"])
            P.act(lambda e, tt=tt: e.activation(out=junk[:], in_=y[tt][:], func=AF.Square, accum_out=ss[:, 0:1]),
                  reads=yk, writes=[f"{tag}junk", f"{tag}ss"])
            emit_rstd(P, tag, ss, rs)
            P.dve(lambda e, tt=tt: e.scalar_tensor_tensor(out=y[tt][:], in0=y[tt][:], scalar=rs[:, 1:2], in1=gb[:],
                                                          op0=ALU.mult, op1=ALU.mult),
                  reads=yk + [f"{tag}rs", f"{tag}gb"], writes=yk)
            P.pool(lambda e, tt=tt: e.tensor_tensor(out=xt[:], in0=xt[:], in1=y[tt][:], op=ALU.add),
                   reads=yk + [f"{tag}xt"], writes=[f"{tag}xt"])
            P.dma("sp", f"{tag}out", lambda e, r0=r0: e.dma_start(out=out_d[r0:r0 + 128, :], in_=xt[:]),
                  reads=[f"{tag}xt"], writes=[(f"{tag}outd", r0)])
            if hT_out is not None:
                P.act(lambda e: e.activation(out=junk[:], in_=xt[:], func=AF.Square, accum_out=ss2[:, 0:1]),
                      reads=[f"{tag}xt"], writes=[f"{tag}junk", f"{tag}n_ss"])
                emit_rstd(P, tag + "n_", ss2, rs2)
                P.dve(lambda e, tt=tt: e.tensor_scalar(out=y[tt][:], in0=xt[:], scalar1=rs2[:, 1:2], scalar2=None, op0=ALU.mult),
                      reads=[f"{tag}xt", f"{tag}n_rs"], writes=yk)
        if hT_out is not None:
            emit_T_block(P, C, y, [[(f"{tag}y", tt, nb) for nb in range(8)] for tt in range(4)], gcol, f"{tag}gcol",
                         mx, mxk, psY[0:4], [(f"{tag}psY", i) for i in range(4)])
            P.dma("sp", f"{tag}hTo", lambda e, ps_=ps_: e.dma_start(out=hT_out[ps_], in_=mx[:]),
                  reads=mxk, writes=[(f"{tag}hTo", ps_)])
    return [f"{tag}out"] + ([f"{tag}hTo"] if hT_out is not None else [])


def build_outproj(TO, with_hT=False):
    nc = bass.Bass("TRN2", target_bir_lowering=False)
    mix_d = nc.dram_tensor("mix", [128, KC, TO], BF16, kind="ExternalInput").ap()
    wo_d = nc.dram_tensor("wo", [8, 2, 128, 16, 512], F32, kind="ExternalInput").ap()
    x_d = nc.dram_tensor("x", [TO, D_MODEL], F32, kind="ExternalInput").ap()
    g_d = nc.dram_tensor("gpost", [128, D_MODEL], F32, kind="ExternalInput").ap()
    out_d = nc.dram_tensor("out", [TO, D_MODEL], F32, kind="ExternalOutput").ap()
    gnext_d = hT_out = None
    if with_hT:
        gnext_d = nc.dram_tensor("gnext", [128, KC], F32, kind="ExternalInput").ap()
        hT_out = nc.dram_tensor("hTn", [TO // 512, 128, KC, 512], BF16, kind="ExternalOutput").ap()
    P = Prog(nc)
    C = make_consts(P)
    fin = emit_outproj(P, C, TO, mix_d, wo_d, x_d, g_d, out_d, gnext_d=gnext_d, hT_out=hT_out)
    P.finalize(final_wait_sems=fin)
    P.close()
    return nc


def lay_w_cols(w, col0):
    blk = w[:, col0:col0 + 128]
    return np.ascontiguousarray(blk.reshape(KC, 128, 128).transpose(1, 0, 2))


def lay_mixer_w(w_in, comp_offsets):
    NH = len(comp_offsets)
    out = np.empty((NH, 4, 128, KC, 128), np.float32)
    for h, offs in enumerate(comp_offsets):
        for c, o in enumerate(offs):
            out[h, c] = lay_w_cols(w_in, o)
    return out


def lay_wout(w_out):
    a = w_out.reshape(2, 16, 128, 8, 512)
    return np.ascontiguousarray(a.transpose(3, 0, 2, 1, 4))


def lay_bias(rel_bias_h):
    s = np.arange(128)[:, None, None]
    dl = np.arange(5)[None, :, None]
    t = np.arange(128)[None, None, :]
    rel = np.clip(t - s + 128 * dl, -128, 128) + 128
    return np.ascontiguousarray(rel_bias_h[rel].reshape(128, 5 * 128))


_CACHE = {}


def _get(name, fn):
    if name not in _CACHE:
        _CACHE[name] = fn()
    return _CACHE[name]


def lay_gcol(g):
    return np.ascontiguousarray(g.reshape(KC, 128).T).astype(np.float32)


def run_mixer_layer(layer, xin, norm_pre_l, w_in, b_f=None, rel_bias=None, hT=None):
    in_maps = []
    if layer == 0:
        types = "AAAABBBB"
        gcol = lay_gcol(norm_pre_l)
        WA = 16 * HD
        for b in range(BATCH):
            for g in range(4):
                offs = []
                for a in range(4):
                    h = 4 * g + a
                    offs.append([h * HD, WA + h * HD, 2 * WA + h * HD, 3 * WA + h * HD])
                for bb in range(4):
                    h = 4 * g + bb
                    offs.append([4 * WA + h * HD, 5 * WA + h * HD, 6 * WA + h * HD, 7 * WA + h * HD])
                waf = w_in[:, 8 * WA + 4 * g: 8 * WA + 4 * g + 4]
                waf = np.ascontiguousarray(waf.reshape(KC, 128, 4).transpose(1, 0, 2))
                in_maps.append({
                    "x": np.ascontiguousarray(xin[b]),
                    "gcol": gcol,
                    "w": lay_mixer_w(w_in, offs),
                    "waf": waf,
                    "bf": np.ascontiguousarray(b_f[4 * g:4 * g + 4].reshape(4, 1)),
                    "biasT": np.stack([lay_bias(rel_bias[4 * g + bb]) for bb in range(4)]),
                })
        nc = _get(("mixer", types), lambda: build_mixer(SEQ, types))
    else:
        types = "CCCCCCCC"
        WC = 32 * HD
        for b in range(BATCH):
            for g in range(4):
                offs = []
                for c in range(8):
                    h = 8 * g + c
                    offs.append([h * HD, WC + h * HD, 2 * WC + h * HD, 3 * WC + h * HD])
                in_maps.append({
                    "hT": np.ascontiguousarray(hT[b]),
                    "w": lay_mixer_w(w_in, offs),
                })
        nc = _get(("mixer", types), lambda: build_mixer(SEQ, types, ext_hT=True))
    res = run_bass_kernel_spmd(nc, in_maps, core_ids=list(range(8)))
    mixT = np.empty((BATCH, KC, 128, SEQ), dtype=ml_dtypes.bfloat16)
    for b in range(BATCH):
        for g in range(4):
            r = np.asarray(res.results[b * 4 + g]["mixT"])
            if layer == 0:
                mixT[b, 4 * g:4 * g + 4] = r[0:4]
                mixT[b, 16 + 4 * g:16 + 4 * g + 4] = r[4:8]
            else:
                mixT[b, 8 * g:8 * g + 8] = r
    return mixT


def run_outproj_layer(mixT, w_out, xin, norm_post_l, norm_pre_next=None):
    gpost = np.ascontiguousarray(np.broadcast_to(norm_post_l[None, :], (128, D_MODEL))).astype(np.float32)
    wo = lay_wout(w_out)
    TO = SEQ // 4
    with_hT = norm_pre_next is not None
    in_maps = []
    for b in range(BATCH):
        for r in range(4):
            m = {
                "mix": np.ascontiguousarray(mixT[b, :, :, r * TO:(r + 1) * TO].transpose(1, 0, 2)),
                "wo": wo,
                "x": np.ascontiguousarray(xin[b, r * TO:(r + 1) * TO]),
                "gpost": gpost,
            }
            if with_hT:
                m["gnext"] = lay_gcol(norm_pre_next)
            in_maps.append(m)
    nc = _get(("outproj", with_hT), lambda: build_outproj(TO, with_hT))
    res = run_bass_kernel_spmd(nc, in_maps, core_ids=list(range(8)))
    out = np.empty((BATCH, SEQ, D_MODEL), np.float32)
    hT = np.empty((BATCH, SEQ // 512, 128, KC, 512), dtype=ml_dtypes.bfloat16) if with_hT else None
    for b in range(BATCH):
        for r in range(4):
            out[b, r * TO:(r + 1) * TO] = np.asarray(res.results[b * 4 + r]["out"])
            if with_hT:
                hT[b, 2 * r:2 * r + 2] = np.asarray(res.results[b * 4 + r]["hTn"])
    return out, hT


def kernel(x, norm_pre, norm_post, w_in_even, b_f_even, rel_bias_even, w_out_even, w_in_odd, w_out_odd):
    x = np.asarray(x, np.float32)
    norm_pre = np.asarray(norm_pre, np.float32)
    norm_post = np.asarray(norm_post, np.float32)
    mix0 = run_mixer_layer(0, x, norm_pre[0], np.asarray(w_in_even[0], np.float32),
                           np.asarray(b_f_even[0], np.float32), np.asarray(rel_bias_even[0], np.float32))
    x1, h1T = run_outproj_layer(mix0, np.asarray(w_out_even[0], np.float32), x, norm_post[0], norm_pre_next=norm_pre[1])
    mix1 = run_mixer_layer(1, None, None, np.asarray(w_in_odd[0], np.float32), hT=h1T)
    out, _ = run_outproj_layer(mix1, np.asarray(w_out_odd[0], np.float32), x1, norm_post[1])
    return out
```
